# Optimizing a Trainium2 kernel written in Bass

```python
import jax, jax.numpy as jnp
from jax import lax
import numpy as np

D_MODEL = 1024
BATCH = 2
SEQ = 8192
DEPTH = 1

CHUNK = 64
POOL_WIDTH = D_MODEL // 2
POOL_WINDOWS = (2, 4, 8, 16)
N_POOL_GROUPS = len(POOL_WINDOWS)
POOL_GROUP = POOL_WIDTH // N_POOL_GROUPS
SB_WIDTH = D_MODEL - POOL_WIDTH
SB_HEAD_DIM = 64
SB_HEADS = SB_WIDTH // SB_HEAD_DIM
Q_BLOCK = 128
D_FF = 4 * D_MODEL
PLE_DIM = 256
IN_WIDTH = POOL_WIDTH + 3 * SB_WIDTH
EPS = 1e-6

kernel_name = "hybrid_pool_stickbreaking_block"


def rms_norm(x, g):
    xf = x.astype(jnp.float32)
    y = xf * lax.rsqrt(jnp.mean(xf * xf, axis=-1, keepdims=True) + EPS)
    return (y * g.astype(jnp.float32)).astype(x.dtype)


def pool_mixer(u, w_pool, scale):
    b, s, _ = u.shape
    uf = u.astype(jnp.float32).reshape(b, s, N_POOL_GROUPS, POOL_GROUP)
    t = jnp.arange(s)
    outs = []
    for gi, w in enumerate(POOL_WINDOWS):
        ug = uf[:, :, gi]
        c = jnp.cumsum(ug, axis=1)
        c_prev = jnp.pad(c, ((0, 0), (w, 0), (0, 0)))[:, :s]
        cnt = jnp.minimum(t + 1, w).astype(jnp.float32)[None, :, None]
        outs.append((c - c_prev) / cnt - ug)
    d = jnp.stack(outs, axis=2).astype(u.dtype)
    y = jnp.einsum('bsgc,gcd->bsgd', d, w_pool).reshape(b, s, POOL_WIDTH)
    return y * scale


def stick_breaking_attention(q, k, v):
    b, s_len, h, dh = q.shape
    scale = dh ** -0.5
    outs = []
    for start in range(0, s_len, Q_BLOCK):
        end = start + Q_BLOCK
        qb = q[:, start:end]
        kb = k[:, :end]
        vb = v[:, :end]
        z = jnp.einsum('bqhd,bkhd->bhqk', qb, kb,
                       preferred_element_type=jnp.float32) * scale
        t_idx = start + jnp.arange(Q_BLOCK)
        s_idx = jnp.arange(end)
        mask = s_idx[None, :] < t_idx[:, None]
        log_fail = jnp.where(mask, jax.nn.log_sigmoid(-z), 0.0)
        after = lax.cumsum(log_fail, axis=3, reverse=True) - log_fail
        log_a = jax.nn.log_sigmoid(z) + after
        a = jnp.where(mask, jnp.exp(log_a), 0.0)
        outs.append(jnp.einsum('bhqk,bkhd->bqhd', a.astype(v.dtype), vb))
    return jnp.concatenate(outs, axis=1)


def setup_inputs(seed: int = 0) -> dict:
    key = jax.random.key(seed)
    ks = jax.random.split(key, 20)
    f32 = jnp.float32

    def nrm(k, shape, fan_in):
        return jax.random.normal(k, shape, f32) * (fan_in ** -0.5)

    def gain(k, shape):
        return 1.0 + 0.05 * jax.random.normal(k, shape, f32)

    return {
        "x": jax.random.normal(ks[0], (BATCH, SEQ, D_MODEL), f32),
        "p": jax.random.normal(ks[1], (DEPTH, BATCH, SEQ, PLE_DIM), f32),
        "g_mix_pre": gain(ks[2], (DEPTH, D_MODEL)),
        "w_in": nrm(ks[3], (DEPTH, D_MODEL, IN_WIDTH), D_MODEL),
        "w_pool": nrm(ks[4], (DEPTH, N_POOL_GROUPS, POOL_GROUP, POOL_GROUP), POOL_GROUP),
        "pool_scale": gain(ks[5], (DEPTH, POOL_WIDTH)),
        "g_sb": gain(ks[6], (DEPTH, SB_WIDTH)),
        "w_out": nrm(ks[7], (DEPTH, D_MODEL, D_MODEL), D_MODEL),
        "g_mix_post": gain(ks[8], (DEPTH, D_MODEL)),
        "g_mlp_pre": gain(ks[9], (DEPTH, D_MODEL)),
        "w_up": nrm(ks[10], (DEPTH, D_MODEL, D_FF), D_MODEL),
        "w_down": nrm(ks[11], (DEPTH, D_FF, D_MODEL), D_FF),
        "g_mlp_post": gain(ks[12], (DEPTH, D_MODEL)),
        "w_ple_gate": nrm(ks[13], (DEPTH, D_MODEL, D_MODEL), D_MODEL),
        "w_ple_proj": nrm(ks[14], (DEPTH, PLE_DIM, D_MODEL), PLE_DIM),
        "g_ple": gain(ks[15], (DEPTH, D_MODEL)),
    }


def reference(x, p, g_mix_pre, w_in, w_pool, pool_scale, g_sb, w_out, g_mix_post,
              g_mlp_pre, w_up, w_down, g_mlp_post, w_ple_gate, w_ple_proj, g_ple):
    b, s, _ = x.shape
    assert s % CHUNK == 0
    h = x
    for i in range(DEPTH):
        hn = rms_norm(h, g_mix_pre[i])
        proj = hn @ w_in[i]
        u = proj[..., :POOL_WIDTH]
        q, k, v = jnp.split(proj[..., POOL_WIDTH:], 3, axis=-1)
        q = q.reshape(b, s, SB_HEADS, SB_HEAD_DIM)
        k = k.reshape(b, s, SB_HEADS, SB_HEAD_DIM)
        v = v.reshape(b, s, SB_HEADS, SB_HEAD_DIM)

        y_pool = pool_mixer(u, w_pool[i], pool_scale[i])
        o_sb = stick_breaking_attention(q, k, v)
        y_sb = rms_norm(o_sb, jnp.ones((SB_HEAD_DIM,), jnp.float32)).reshape(b, s, SB_WIDTH) * g_sb[i]

        mix = jnp.concatenate([y_pool, y_sb], axis=-1) @ w_out[i]
        h = h + rms_norm(mix, g_mix_post[i])

        m = rms_norm(h, g_mlp_pre[i]) @ w_up[i]
        m = jnp.square(jax.nn.relu(m)) @ w_down[i]
        h = h + rms_norm(m, g_mlp_post[i])

        gate = jax.nn.sigmoid(h @ w_ple_gate[i])
        e = rms_norm(p[i] @ w_ple_proj[i], g_ple[i])
        h = h + gate * e
    return h
```

```python
import contextlib
import numpy as np
import concourse.bass as bass
import concourse.mybir as mybir
from concourse.bass_utils import run_bass_kernel_spmd

F32 = mybir.dt.float32
BF16 = mybir.dt.bfloat16
AF = mybir.ActivationFunctionType
ALU = mybir.AluOpType

D = 1024
S_LEN = 8192
NB = 64
DFF = 4096
PLE = 256
EPS = 1e-6
NEG = -30000.0
ENGS = ("pe", "act", "dve", "pool", "sp")
SYNC_SAME_ENGINE = True
SUB_ENG = "pool"


class _Op:
    __slots__ = ("eng", "fn", "deps", "signal", "ticket", "sem", "dma", "n_dma")

    def __init__(self, eng, fn, dma, n_dma):
        self.eng = eng
        self.fn = fn
        self.deps = []
        self.signal = False
        self.ticket = None
        self.sem = None
        self.dma = dma
        self.n_dma = n_dma


class Sched:
    def __init__(self):
        self.ops = {e: [] for e in ENGS}
        self.last_w = {}
        self.readers = {}
        self.dma_keys = []
        self.final_deps = []
        self._dsem = {}

    def op(self, eng, fn, reads=(), writes=(), dma=None, n_dma=1):
        o = _Op(eng, fn, dma, n_dma)
        if dma is not None and dma not in self.dma_keys:
            self.dma_keys.append(dma)
        deps = []
        for r in reads:
            w = self.last_w.get(r)
            if w is not None:
                deps.append((w, "raw"))
        for r in writes:
            w = self.last_w.get(r)
            if w is not None:
                deps.append((w, "waw"))
            for rd in self.readers.get(r, ()):
                deps.append((rd, "war"))
        for d, kind in deps:
            if d is o:
                continue
            if d.eng == eng and d.dma is None and dma is None and kind != "raw" and not SYNC_SAME_ENGINE:
                continue
            if d.eng == "pe" and eng == "pe" and d.dma is None and dma is None:
                continue
            o.deps.append(d)
            d.signal = True
        for r in reads:
            self.readers.setdefault(r, []).append(o)
        for r in writes:
            self.last_w[r] = o
            self.readers[r] = []
        self.ops[eng].append(o)
        return o

    def dma(self, eng, key, pairs, reads=(), writes=()):
        sched = self

        def fn(e, pairs=pairs, key=key):
            inst = None
            for (o_ap, i_ap) in pairs:
                inst = e.dma_start(out=o_ap, in_=i_ap)
                inst.then_inc(sched._dsem[key], 16)
            return inst

        return self.op(eng, fn, reads=reads, writes=writes, dma=key, n_dma=len(pairs))

    def finish(self, ops):
        for o in ops:
            o.signal = True
            self.final_deps.append(o)

    def emit(self, nc, stack):
        esem = {e: stack.enter_context(nc.semaphore("s_" + e)) for e in ENGS}
        dsem = {k: stack.enter_context(nc.semaphore("d_%d" % i))
                for i, k in enumerate(self.dma_keys)}
        self._dsem = dsem
        dcount = {k: 0 for k in self.dma_keys}
        for e in ENGS:
            c = 0
            for o in self.ops[e]:
                if o.dma is not None:
                    dcount[o.dma] += 16 * o.n_dma
                    o.ticket = dcount[o.dma]
                    o.sem = dsem[o.dma]
                elif o.signal:
                    c += 1
                    o.ticket = c
                    o.sem = esem[e]
        final_deps = self.final_deps
        ops = self.ops

        def collect(deps):
            need = {}
            for d in deps:
                key = d.sem.num
                if need.get(key, (None, 0))[1] < d.ticket:
                    need[key] = (d.sem, d.ticket)
            return need

        def run(eng_name, eng):
            known = {}
            for o in ops[eng_name]:
                for key, (sem, val) in collect(o.deps).items():
                    if known.get(key, 0) >= val:
                        continue
                    eng.wait_ge(sem, val)
                    known[key] = val
                inst = o.fn(eng)
                if o.dma is None and o.signal:
                    inst.then_inc(o.sem, 1)
            if eng_name == "sp":
                for key, (sem, val) in collect(final_deps).items():
                    eng.wait_ge(sem, val)

        with nc.Block() as block:
            @block.tensor
            def _(e):
                run("pe", e)

            @block.scalar
            def _(e):
                run("act", e)

            @block.vector
            def _(e):
                run("dve", e)

            @block.gpsimd
            def _(e):
                run("pool", e)

            @block.sync
            def _(e):
                run("sp", e)


def build_program(n_groups=16, do_c=True, debug=False):
    nc = bass.Bass("TRN2", target_bir_lowering=False)

    def din(name, shape, dt=F32):
        return nc.dram_tensor(name, shape, dt, kind="ExternalInput").ap()

    xk = din("xk", [S_LEN, D])
    p_own = din("p_own", [2048, PLE])
    w_in = din("w_in", [D, 2048])
    w_pool = din("w_pool", [128, 512])
    w_out = din("w_out", [D, D])
    w_up = din("w_up", [D, DFF])
    w_down = din("w_down", [DFF, D])
    w_gate = din("w_gate", [D, D])
    w_ple = din("w_ple", [PLE, D])
    gcols = din("gcols", [128, 24])
    g_post = din("g_post", [3, D])
    consts = din("consts", [128, 512])
    rc16 = din("rc16", [128, 64])
    y_out = nc.dram_tensor("y_out", [2048, D], F32, kind="ExternalOutput").ap()
    yscr = nc.dram_tensor("yscr", [16, 128, 1024], BF16,
                          kind=("ExternalOutput" if debug else "Internal")).ap()

    S = Sched()
    with contextlib.ExitStack() as st:
        def sb(name, shape, dt):
            return st.enter_context(nc.sbuf_tensor(name, shape, dt))

        BIG = sb("BIG", [128, 65536], BF16)
        WIN = sb("WIN", [128, 16384], BF16)
        F4 = [sb("F4a", [128, 1024], F32), sb("F4b", [128, 1024], F32)]
        HNT = sb("HNT", [128, 4096], BF16)
        GB = sb("GB", [128, 1548], F32)
        SBF = sb("SBF", [128, 1032], F32)
        ABF = sb("ABF", [128, 1560], BF16)
        ATS = sb("ATS", [128, 1024], BF16)
        QTB = sb("QTB", [128, 1024], BF16)
        YP1 = sb("YP1", [128, 512], BF16)
        UT = sb("UT", [128, 576], F32)
        T12 = sb("T12", [128, 288], F32)
        DT = sb("DT", [128, 512], BF16)
        YH = sb("YH", [128, 2048], BF16)
        CST = sb("CST", [128, 512], BF16)
        WPL = sb("WPL", [128, 512], BF16)
        RC = sb("RC", [128, 64], F32)
        GC = sb("GC", [128, 24], F32)
        ST = sb("ST", [128, 32], F32)
        ONE = sb("ONE", [128, 1], F32)
        banks = [st.enter_context(nc.psum_tensor("B%d" % i, [128, 512], F32)) for i in range(8)]

        KT = BIG[:, 0:32768].rearrange("p (a t) -> p a t", a=4)
        V = BIG[:, 32768:65536].rearrange("p (a c) -> p a c", a=64)
        WUP = BIG[:, 0:32768].rearrange("p (a c) -> p a c", a=8)
        WDN = BIG[:, 32768:65536].rearrange("p (a c) -> p a c", a=32)
        WINV = WIN[:, :].rearrange("p (a c) -> p a c", a=8)
        WOUT = WIN[:, 0:8192].rearrange("p (a c) -> p a c", a=8)
        WGT = WIN[:, 8192:16384].rearrange("p (a c) -> p a c", a=8)
        hnT = HNT[:, :].rearrange("p (a t) -> p a t", a=8)
        yh = HNT[:, 0:2048].rearrange("p (j c t) -> p j c t", j=2, c=8)
        hnT2 = HNT[:, 2048:4096].rearrange("p (a t) -> p a t", a=8)
        gb = [GB[:, k * 516:k * 516 + 513] for k in range(3)]
        sbf = [SBF[:, 0:513], SBF[:, 516:1029]]
        sh = [ABF[:, k * 520:k * 520 + 513] for k in range(3)]
        sq0 = ABF[:, 0:512]
        ats = [ATS[:, 0:512], ATS[:, 512:1024]]
        QT2 = [QTB[:, k * 512:(k + 1) * 512].rearrange("p (a t) -> p a t", a=4) for k in range(2)]
        uT = UT[:, :].rearrange("p (a t) -> p a t", a=4)
        tt = [T12[:, 0:144], T12[:, 144:288]]
        dT = DT[:, :].rearrange("p (a t) -> p a t", a=4)
        ypool = [YH[:, 0:512], YP1[:, :]]
        ysb = YH[:, 512:1024]
        sqT = DT[:, :]
        yT3 = YH[:, 0:1024].rearrange("p (a t) -> p a t", a=8)
        hn = YH[:, 1024:2048]
        WPLE = YH[:, :].rearrange("p (a c) -> p a c", a=2)
        ident = CST[:, 0:128]
        negmask = CST[:, 128:256]
        blockones = CST[:, 256:384]
        negident = CST[:, 384:512]
        wpool = WPL[:, :].rearrange("p (a d) -> p a d", a=4)
        rc = RC[:, :].rearrange("p (a t) -> p a t", a=4)
        Bbf = [b[:, :].bitcast(BF16) for b in banks]
        tmpC = GB[:, 0:1024]
        gainC = SBF[:, 0:1024]
        hnC = ABF[:, 0:1024]
        actT = ATS[:, :].rearrange("p (a t) -> p a t", a=4)
        pbC = QTB[:, 0:256]
        pTC = QTB[:, 256:512].rearrange("p (a t) -> p a t", a=2)
        rC = [UT[:, 0:256], UT[:, 256:512]]
        pC = T12[:, 0:256]

        S.dma("pool", "cst", [(CST[:, :], consts)], writes=["cst"])
        S.dma("pool", "wpl", [(WPL[:, :], w_pool)], writes=["wpool"])
        S.dma("sp", "rc", [(RC[:, :], rc16)], writes=["rc"])
        S.dma("sp", "gc", [(GC[:, :], gcols)], writes=["gc"])
        S.op("pool", lambda e: e.memset(ONE[:, :], 1.0), writes=["one"])
        S.op("pool", lambda e: e.memset(GB[:, 0:1], 1.0), writes=["gb0"])
        S.op("pool", lambda e: e.memset(GB[:, 516:517], 1.0), writes=["gb1"])
        S.op("pool", lambda e: e.memset(GB[:, 1032:1033], 1.0), writes=["gb2"])

        for kc in range(8):
            for hf in range(2):
                i = kc * 2 + hf
                buf = F4[i % 2]
                S.dma("sp", "f4%d" % (i % 2),
                      [(buf[:, :], w_in[kc * 128:(kc + 1) * 128, hf * 1024:(hf + 1) * 1024])],
                      writes=["f4%d" % (i % 2)])
                eng = "act" if i % 2 == 0 else "dve"
                if eng == "act":
                    S.op("act", lambda e, kc=kc, hf=hf, buf=buf: e.activation(
                        out=WINV[:, kc, hf * 1024:(hf + 1) * 1024], in_=buf[:, :], func=AF.Copy,
                        scale=GC[:, kc:kc + 1]),
                        reads=["f4%d" % (i % 2), "gc"], writes=["win"])
                else:
                    S.op("dve", lambda e, kc=kc, hf=hf, buf=buf: e.tensor_scalar(
                        out=WINV[:, kc, hf * 1024:(hf + 1) * 1024], in0=buf[:, :],
                        scalar1=GC[:, kc:kc + 1], scalar2=None, op0=ALU.mult),
                        reads=["f4%d" % (i % 2), "gc"], writes=["win"])

        I32 = mybir.dt.int32
        STI = ST[:, :].bitcast(I32)

        def rstd_thunks(ss_col, c0, scale, tag, iters):
            y, a_, t_, u_ = c0, c0 + 1, c0 + 2, c0 + 3
            col = lambda c: ST[:, c:c + 1]
            coli = lambda c: STI[:, c:c + 1]
            th = []
            th.append(lambda: S.op("dve", lambda e: e.tensor_scalar(
                out=col(a_), in0=col(ss_col), scalar1=scale, scalar2=EPS, op0=ALU.mult, op1=ALU.add),
                reads=["ss" + tag], writes=["a" + tag]))
            th.append(lambda: S.op("dve", lambda e: e.tensor_scalar(
                out=coli(y), in0=coli(a_), scalar1=1, scalar2=None, op0=ALU.logical_shift_right),
                reads=["a" + tag], writes=["r" + tag]))
            th.append(lambda: S.op("dve", lambda e: e.tensor_scalar(
                out=coli(y), in0=coli(y), scalar1=-1, scalar2=0x5f3759df, op0=ALU.mult, op1=ALU.add),
                reads=["r" + tag], writes=["r" + tag]))
            for _ in range(iters):
                th.append(lambda: S.op("dve", lambda e: e.scalar_tensor_tensor(
                    out=col(t_), in0=col(y), scalar=col(a_), in1=col(y), op0=ALU.mult, op1=ALU.mult),
                    reads=["r" + tag, "a" + tag], writes=["t" + tag]))
                th.append(lambda: S.op("dve", lambda e: e.scalar_tensor_tensor(
                    out=col(u_), in0=col(t_), scalar=-0.5, in1=col(y), op0=ALU.mult, op1=ALU.mult),
                    reads=["t" + tag, "r" + tag], writes=["u" + tag]))
                th.append(lambda: S.op("dve", lambda e: e.scalar_tensor_tensor(
                    out=col(y), in0=col(y), scalar=1.5, in1=col(u_), op0=ALU.mult, op1=ALU.add),
                    reads=["u" + tag, "r" + tag], writes=["r" + tag]))
            return th

        def rstd_act(ss_col, t_col, r_col, scale, tag):
            S.op("dve", lambda e: e.tensor_scalar(out=ST[:, t_col:t_col + 1], in0=ST[:, ss_col:ss_col + 1],
                                                  scalar1=scale, scalar2=EPS, op0=ALU.mult, op1=ALU.add),
                 reads=["ss" + tag], writes=["t" + tag])
            S.op("act", lambda e: e.activation(out=ST[:, t_col:t_col + 1], in_=ST[:, t_col:t_col + 1],
                                               func=AF.Sqrt),
                 reads=["t" + tag], writes=["t" + tag])
            S.op("dve", lambda e: e.reciprocal(out=ST[:, r_col:r_col + 1], in_=ST[:, t_col:t_col + 1]),
                 reads=["t" + tag], writes=["r" + tag])

        xcnt = [0]
        step_ctr = [0]

        junkA = SBF[:, 516:1028].bitcast(BF16)

        def proj_kv(m):
            out = []
            tag = "A"
            for bi in range(4):
                pos = 4 * m + bi
                xi = xcnt[0] % 2
                xcnt[0] += 1
                xt = F4[xi]
                xn = "f4%d" % xi
                out.append(lambda xt=xt, xn=xn, pos=pos: S.dma(
                    "sp", xn, [(xt[:, :], xk[pos * 128:(pos + 1) * 128, :])], writes=[xn]))
                out.append(lambda xt=xt, xn=xn, bi=bi: S.op(
                    "act", lambda e: e.activation(out=junkA, in_=xt[:, :], func=AF.Square,
                                                  accum_out=ST[:, bi:bi + 1]),
                    reads=[xn], writes=["junkA", "ssA%d" % bi]))
            out.append(lambda: S.op(
                "dve", lambda e: e.tensor_scalar(out=ST[:, 28:32], in0=ST[:, 0:4], scalar1=1.0 / D, scalar2=EPS,
                                                 op0=ALU.mult, op1=ALU.add),
                reads=["ssA0", "ssA1", "ssA2", "ssA3"], writes=["tA"]))
            out.append(lambda: S.op(
                "act", lambda e: e.activation(out=ST[:, 28:32], in_=ST[:, 28:32], func=AF.Sqrt),
                reads=["tA"], writes=["tA"]))
            out.append(lambda: S.op(
                "dve", lambda e: e.reciprocal(out=ST[:, 24:28], in_=ST[:, 28:32]),
                reads=["tA"], writes=["rA"]))
            per_block = []
            for bi in range(4):
                th = []
                pos = 4 * m + bi
                xi = xcnt[0] % 2
                xcnt[0] += 1
                xt = F4[xi]
                xn = "f4%d" % xi
                th.append(lambda xt=xt, xn=xn, pos=pos: S.dma(
                    "sp", xn, [(xt[:, :], xk[pos * 128:(pos + 1) * 128, :])], writes=[xn]))
                th.append(lambda xt=xt, xn=xn, bi=bi: S.op(
                    "act", lambda e: e.activation(out=hn, in_=xt[:, :], func=AF.Copy, scale=ST[:, 24 + bi:25 + bi]),
                    reads=[xn, "rA"], writes=["hn"]))

                for h2 in range(2):
                    def tr4(e, h2=h2):
                        inst = None
                        for kc in range(4 * h2, 4 * h2 + 4):
                            inst = e.transpose(Bbf[0][:, kc * 128:(kc + 1) * 128],
                                               hn[:, kc * 128:(kc + 1) * 128], ident)
                        return inst
                    th.append(lambda tr4=tr4: S.op("pe", tr4, reads=["hn", "cst"], writes=["b0"]))
                th.append(lambda bi=bi: S.op(
                    "dve", lambda e: e.tensor_copy(out=hnT[:, :, bi * 128:(bi + 1) * 128],
                                                   in_=Bbf[0][:, :].rearrange("p (a t) -> p a t", a=8)),
                    reads=[], writes=["b0", "hnT%d" % bi]))
                for q4 in range(4):
                    def vmm(e, bi=bi, q4=q4):
                        inst = None
                        for kc in range(2 * q4, 2 * q4 + 2):
                            inst = e.matmul(banks[1][:, :], lhsT=hnT[:, kc, bi * 128:(bi + 1) * 128],
                                            rhs=WINV[:, kc, 1536:2048], start=(kc == 0), stop=(kc == 7))
                        return inst
                    th.append(lambda vmm=vmm, bi=bi: S.op("pe", vmm, reads=["hnT%d" % bi, "win"], writes=["b1"]))
                th.append(lambda pos=pos, m=m: S.op(
                    "act", lambda e: e.activation(out=V[:, pos, :], in_=banks[1][:, :], func=AF.Copy),
                    reads=[], writes=["b1", "V%d" % m]))
                per_block.append(th)
            stagger = 6
            keyed = []
            for bi, th in enumerate(per_block):
                for k, t_ in enumerate(th):
                    keyed.append((k + bi * stagger, bi, k, t_))
            keyed.sort(key=lambda z: (z[0], z[1]))
            out += [z[3] for z in keyed]
            allT = ["hnT%d" % bi for bi in range(4)]
            for pr in range(4):
                for q4 in range(4):
                    def kmm(e, pr=pr, q4=q4):
                        inst = None
                        for kc in range(2 * q4, 2 * q4 + 2):
                            inst = e.matmul(banks[1][:, :],
                                            lhsT=WINV[:, kc, 1024 + pr * 128:1024 + (pr + 1) * 128],
                                            rhs=hnT[:, kc, :], start=(kc == 0), stop=(kc == 7))
                        return inst
                    out.append(lambda kmm=kmm: S.op("pe", kmm, reads=allT + ["win"], writes=["b1"]))
                out.append(lambda pr=pr, m=m: S.op(
                    "dve", lambda e: e.tensor_copy(out=KT[:, pr, m * 512:(m + 1) * 512], in_=banks[1][:, :]),
                    reads=[], writes=["b1", "KT%d" % m]))
            return out

        def proj_own(m):
            qp = m % 2
            th = []

            def t_qp(pr):
                def qmm(e):
                    inst = None
                    for kc in range(8):
                        inst = e.matmul(banks[1][:, pr * 128:(pr + 1) * 128],
                                        lhsT=WINV[:, kc, 512 + pr * 128:512 + (pr + 1) * 128],
                                        rhs=hnT[:, kc, 0:128], start=(kc == 0), stop=(kc == 7))
                    return inst
                S.op("pe", qmm, reads=["hnT0", "win"], writes=["b1"])
            for pr in range(4):
                th.append(lambda pr=pr: t_qp(pr))
            th.append(lambda: S.op(
                "dve", lambda e: e.tensor_copy(out=QTB[:, qp * 512:(qp + 1) * 512], in_=banks[1][:, :]),
                reads=[], writes=["b1", "QT%d" % qp]))

            def t_u(rd):
                def umm(e):
                    inst = None
                    for q in range(2):
                        gi = rd * 2 + q
                        for kc in range(8):
                            inst = e.matmul(banks[1][:, q * 144:(q + 1) * 144],
                                            lhsT=WINV[:, kc, gi * 128:(gi + 1) * 128],
                                            rhs=hnT[:, kc, 0:144], start=(kc == 0), stop=(kc == 7))
                    return inst
                S.op("pe", umm, reads=["hnT0", "hnT1", "win"], writes=["b1"])
                S.op("act", lambda e: e.activation(out=UT[:, rd * 288:(rd + 1) * 288],
                                                   in_=banks[1][:, 0:288], func=AF.Copy),
                     reads=[], writes=["b1", "uT"])
            th.append(lambda: t_u(0))
            th.append(lambda: t_u(1))

            def t_mix(gi):
                a_ = uT[:, gi, :]
                w = 2 ** (gi + 1)
                S.op("dve", lambda e: e.tensor_tensor(out=tt[0][:, 0:143], in0=a_[:, 0:143],
                                                      in1=a_[:, 1:144], op=ALU.add),
                     reads=["uT"], writes=["tt0"])
                cur, ln, sh_ = 0, 143, 2
                for lvl in range(gi):
                    nl = ln - sh_
                    S.op("dve", lambda e, cur=cur, nl=nl, sh_=sh_: e.tensor_tensor(
                        out=tt[1 - cur][:, 0:nl], in0=tt[cur][:, 0:nl], in1=tt[cur][:, sh_:sh_ + nl], op=ALU.add),
                        reads=["tt%d" % cur], writes=["tt%d" % (1 - cur)])
                    cur, ln, sh_ = 1 - cur, nl, sh_ * 2
                S.op("dve", lambda e, cur=cur: e.scalar_tensor_tensor(
                    out=dT[:, gi, :], in0=tt[cur][:, 0:128], scalar=1.0 / w, in1=a_[:, 0:128],
                    op0=ALU.mult, op1=ALU.subtract),
                    reads=["tt%d" % cur, "uT"], writes=["dT"])
                if m == 15:
                    S.op("dve", lambda e, cur=cur: e.tensor_tensor(
                        out=tt[1 - cur][:, 0:16], in0=tt[cur][:, 112:128], in1=rc[:, gi, :], op=ALU.mult),
                        reads=["tt%d" % cur, "rc"], writes=["tt%d" % (1 - cur)])
                    S.op("dve", lambda e, cur=cur: e.tensor_tensor(
                        out=dT[:, gi, 112:128], in0=tt[1 - cur][:, 0:16], in1=a_[:, 112:128], op=ALU.subtract),
                        reads=["tt%d" % (1 - cur), "uT"], writes=["dT"])
            for gi in range(4):
                th.append(lambda gi=gi: t_mix(gi))

            def t_p():
                def pmm(e):
                    inst = None
                    for gi in range(4):
                        inst = e.matmul(banks[1][:, gi * 128:(gi + 1) * 128], lhsT=wpool[:, gi, :],
                                        rhs=dT[:, gi, :], start=True, stop=True)
                    return inst
                S.op("pe", pmm, reads=["dT", "wpool"], writes=["b1"])
                S.op("act", lambda e: e.activation(out=ypool[qp], in_=banks[1][:, :], func=AF.Copy),
                     reads=[], writes=["b1", "yp%d" % qp])
            th.append(t_p)
            return th

        def attention(m, bg):
            sp_ = (15 - m) % 2
            oT = banks[6 + sp_]
            oTn = "oT%d" % sp_
            nch = 16 - m
            steps = []
            for pr in range(4):
                for c in range(nch):
                    for hh in range(2):
                        steps.append((pr, hh, c))
            K = len(steps)
            base = step_ctr[0]
            step_ctr[0] += K
            QT = QT2[m % 2]
            QTn = "QT%d" % (m % 2)

            def emit_qk(i):
                pr, hh, c = steps[i]
                par = (base + i) % 2
                g3 = (base + i) % 3
                lo, hi = hh * 64, hh * 64 + 64
                k0 = (4 * m + 4 * c) * 128

                def f(e):
                    inst = e.matmul(banks[2 + par][:, :], lhsT=QT[lo:hi, pr, :], rhs=KT[lo:hi, pr, k0:k0 + 512],
                                    start=True, stop=(c != 0))
                    if c == 0:
                        inst = e.matmul(banks[2 + par][:, 0:128], lhsT=ident, rhs=negmask,
                                        start=False, stop=True)
                    return inst
                S.op("pe", f, reads=[QTn, "KT%d" % (m + c), "cst"], writes=["z%d" % par])
                S.op("act", lambda e: e.activation(out=gb[g3][:, 1:513], in_=banks[2 + par][:, :],
                                                   func=AF.Sigmoid, scale=-0.125),
                     reads=[], writes=["z%d" % par, "gb%d" % g3])

            def emit_scan(i):
                pr, hh, c = steps[i]
                g3 = (base + i) % 3
                if c == 0:
                    init = 1.0
                    rds = ["gb%d" % g3, "one"]
                else:
                    p3 = (base + i - 2) % 3
                    init = sh[p3][:, 512:513]
                    rds = ["gb%d" % g3, "one", "sh%d" % p3]
                S.op("dve", lambda e: e.tensor_tensor_scan(
                    out=sh[g3], data0=gb[g3], data1=ONE[:, 0:1].to_broadcast([128, 513]),
                    initial=init, op0=ALU.mult, op1=ALU.mult),
                    reads=rds, writes=["sh%d" % g3])

            def emit_tr(i):
                par = (base + i) % 2
                s3 = (base + i) % 3

                def f(e):
                    inst = None
                    for j in range(4):
                        e.matmul(banks[4 + par][:, j * 128:(j + 1) * 128], lhsT=sh[s3][:, j * 128:j * 128 + 128],
                                 rhs=ident, start=True, stop=False)
                        inst = e.matmul(banks[4 + par][:, j * 128:(j + 1) * 128],
                                        lhsT=sh[s3][:, j * 128 + 1:j * 128 + 129],
                                        rhs=negident, start=False, stop=True)
                    return inst
                S.op("pe", f, reads=["sh%d" % s3, "cst"], writes=["atp%d" % par])
                S.op("act", lambda e: e.activation(out=ats[par], in_=banks[4 + par][:, :], func=AF.Copy),
                     reads=[], writes=["atp%d" % par, "ats%d" % par])

            def emit_av(i):
                pr, hh, c = steps[i]
                par = (base + i) % 2
                head = 2 * pr + hh
                lo, hi = hh * 64, hh * 64 + 64

                def f(e):
                    inst = None
                    for j in range(4):
                        pos = 4 * m + 4 * c + j
                        inst = e.matmul(oT[lo:hi, pr * 128:(pr + 1) * 128],
                                        lhsT=V[:, pos, head * 64:(head + 1) * 64],
                                        rhs=ats[par][:, j * 128:(j + 1) * 128],
                                        start=(c == 0 and j == 0), stop=(c == nch - 1 and j == 3))
                    return inst
                S.op("pe", f, reads=["ats%d" % par, "V%d" % (m + c)], writes=[oTn])

            n_it = K + 5
            done = 0
            for it, i in enumerate(range(-3, K + 2)):
                if 0 <= i < K:
                    emit_scan(i)
                if 0 <= i + 3 < K:
                    emit_qk(i + 3)
                if 0 <= i - 1 < K:
                    emit_tr(i - 1)
                if 0 <= i - 2 < K:
                    emit_av(i - 2)
                tgt = (len(bg) * (it + 1)) // n_it
                while done < tgt:
                    bg[done]()
                    done += 1
            while done < len(bg):
                bg[done]()
                done += 1

        def tail_thunks(m):
            sp_ = (15 - m) % 2
            oT = banks[6 + sp_]
            oTn = "oT%d" % sp_
            qp = m % 2
            th = []
            th.append(lambda: S.op("act", lambda e: e.activation(out=sqT, in_=oT[:, :], func=AF.Square),
                                   reads=[], writes=[oTn, "dT"]))
            th.append(lambda: S.op("pe", lambda e: e.matmul(banks[1][:, :], lhsT=blockones, rhs=sqT,
                                                            start=True, stop=True),
                                   reads=["dT", "cst"], writes=["b1"]))
            th.append(lambda: S.op("dve", lambda e: e.tensor_scalar(out=sbf[0][:, 0:512], in0=banks[1][:, :],
                                                                    scalar1=EPS, scalar2=None, op0=ALU.add),
                                   reads=[], writes=["b1", "sb0"]))
            th.append(lambda: S.op("act", lambda e: e.activation(out=sbf[0][:, 0:512], in_=sbf[0][:, 0:512],
                                                                 func=AF.Sqrt),
                                   reads=["sb0"], writes=["sb0"]))
            th.append(lambda: S.op("dve", lambda e: e.reciprocal(out=sbf[0][:, 0:512], in_=sbf[0][:, 0:512]),
                                   reads=["sb0"], writes=["sb0"]))
            th.append(lambda: S.op("dve", lambda e: e.tensor_tensor(out=ysb, in0=oT[:, :],
                                                                    in1=sbf[0][:, 0:512], op=ALU.mult),
                                   reads=["sb0"], writes=[oTn, "ysb"]))
            th.append(lambda: S.dma("sp", "yscr", [(yscr[m][:, 0:512], ypool[qp]), (yscr[m][:, 512:1024], ysb)],
                                    reads=["yp%d" % qp, "ysb"], writes=["yscr%d" % m]))
            return th

        stg = [0]

        def stage(dst_view, src_ap, gcol, wname, extra=()):
            i = stg[0]
            stg[0] += 1
            buf = F4[i % 2]
            bn = "f4%d" % (i % 2)
            S.dma("sp", bn, [(buf[:, :], src_ap)], writes=[bn])
            if i % 2 == 0:
                S.op("act", lambda e: e.activation(out=dst_view, in_=buf[:, :], func=AF.Copy,
                                                   scale=GC[:, gcol:gcol + 1]),
                     reads=[bn, "gc"], writes=[wname] + list(extra))
            else:
                S.op("dve", lambda e: e.tensor_scalar(out=dst_view, in0=buf[:, :], scalar1=GC[:, gcol:gcol + 1],
                                                      scalar2=None, op0=ALU.mult),
                     reads=[bn, "gc"], writes=[wname] + list(extra))

        def win_loads():
            th = []
            for kc in range(8):
                th.append(lambda kc=kc: stage(WOUT[:, kc, :], w_out[kc * 128:(kc + 1) * 128, :], 8 + kc, "win"))
            th.append(lambda: S.dma(
                "pool", "wgt", [(WGT[:, kc * 4:(kc + 1) * 4, :],
                                 w_gate[kc * 512:(kc + 1) * 512, :].rearrange("(c p) n -> p c n", p=128))
                                for kc in range(2)], writes=["win"]))
            return th

        m_lo = 16 - n_groups
        for t_ in proj_kv(15) + proj_own(15):
            t_()
        win_done = False
        prev_tail = []
        for m_ in range(15, m_lo - 1, -1):
            bg = list(prev_tail)
            if m_ - 1 >= m_lo:
                bg += proj_kv(m_ - 1) + proj_own(m_ - 1)
            elif do_c:
                bg += win_loads()
                win_done = True
            attention(m_, bg)
            prev_tail = tail_thunks(m_)
        for t_ in prev_tail:
            t_()

        if not do_c:
            o = S.dma("sp", "yout", [(y_out[0:128, 0:512], UT[:, 0:512])], reads=["uT"])
            S.finish([o])
            S.emit(nc, st)
            return nc

        allKV = ["KT%d" % i for i in range(16)] + ["V%d" % i for i in range(16)]
        if not win_done:
            for t_ in win_loads():
                t_()
        S.dma("pool", "wple", [(WPLE[:, :, :], w_ple.rearrange("(c p) n -> p c n", p=128))],
              reads=[], writes=["yp0", "ysb", "hn"])
        for kc in range(8):
            for q in range(4):
                stage(WUP[:, kc, q * 1024:(q + 1) * 1024],
                      w_up[kc * 128:(kc + 1) * 128, q * 1024:(q + 1) * 1024], 16 + kc, "wup",
                      extra=(allKV if (kc == 0 and q < 2) else ()))
        S.dma("pool", "wdn", [(WDN[:, c4 * 4:(c4 + 1) * 4, :],
                               w_down[c4 * 512:(c4 + 1) * 512, :].rearrange("(c p) n -> p c n", p=128))
                              for c4 in range(8)], writes=allKV + ["wdn"])

        outs = []
        gidx = {"mix": 0, "mlp": 1, "ple": 2}

        cur_gain = [None]

        def load_gain(which):
            if cur_gain[0] == which:
                return
            cur_gain[0] = which
            S.dma("sp", "gain", [(gainC, g_post[gidx[which]:gidx[which] + 1, :].partition_broadcast(128))],
                  writes=["gain"])

        def norm_residual(src_banks, xt, xn, which, tagc):
            for hf in range(2):
                S.op("act", lambda e, hf=hf: e.activation(out=hnC[:, hf * 512:(hf + 1) * 512],
                                                          in_=banks[src_banks[hf]][:, :], func=AF.Square,
                                                          accum_out=ST[:, 4 + hf:5 + hf]),
                     reads=[], writes=["b%d" % src_banks[hf], "hnC", "ssq%d" % hf])
            S.op("dve", lambda e: e.tensor_tensor(out=ST[:, 6:7], in0=ST[:, 4:5], in1=ST[:, 5:6], op=ALU.add),
                 reads=["ssq0", "ssq1"], writes=["ssN"])
            rstd_act(6, 7, 8, 1.0 / D, "N")
            load_gain(which)
            for hf in range(2):
                S.op("dve", lambda e, hf=hf: e.scalar_tensor_tensor(
                    out=tmpC[:, hf * 512:(hf + 1) * 512], in0=banks[src_banks[hf]][:, :], scalar=ST[:, 8:9],
                    in1=gainC[:, hf * 512:(hf + 1) * 512], op0=ALU.mult, op1=ALU.mult),
                    reads=["rN", "gain"], writes=["b%d" % src_banks[hf], "tmpC"])
            if xt is not None:
                S.op("dve", lambda e: e.tensor_tensor(out=xt[:, :], in0=xt[:, :], in1=tmpC, op=ALU.add),
                     reads=["tmpC", xn], writes=[xn])

        def to_T(src_ap, dst3, nk, src_name, dst_name):
            def f(e):
                inst = None
                for kc in range(nk):
                    inst = e.transpose(Bbf[0][:, kc * 128:(kc + 1) * 128], src_ap[:, kc * 128:(kc + 1) * 128], ident)
                return inst
            S.op("pe", f, reads=[src_name, "cst"], writes=["b0"])
            S.op("dve", lambda e: e.tensor_copy(
                out=dst3, in_=Bbf[0][:, 0:nk * 128].rearrange("p (a t) -> p a t", a=nk)),
                reads=[], writes=["b0", dst_name])

        for t in range(8):
            accb = [[4, 5], [6, 7]]
            for j in range(2):
                mblk = 2 * t + j
                S.dma("sp", "yh%d" % j, [(HNT[:, j * 1024:(j + 1) * 1024], yscr[mblk])],
                      reads=["yscr%d" % mblk], writes=["yh%d" % j])
                S.dma("sp", "f4%d" % j, [(F4[j][:, :], xk[(4 * mblk) * 128:(4 * mblk + 1) * 128, :])],
                      writes=["f4%d" % j])
            for j in range(2):
                def f(e, j=j):
                    inst = None
                    for hf in range(2):
                        for kc in range(8):
                            inst = e.matmul(banks[accb[j][hf]][:, :], lhsT=yh[:, j, kc, :],
                                            rhs=WOUT[:, kc, hf * 512:(hf + 1) * 512],
                                            start=(kc == 0), stop=(kc == 7))
                    return inst
                S.op("pe", f, reads=["yh%d" % j, "win"], writes=["b%d" % accb[j][0], "b%d" % accb[j][1]])
                norm_residual(accb[j], F4[j], "f4%d" % j, "mix", "m")
            for j in range(2):
                S.op("act", lambda e, j=j: e.activation(out=hnC, in_=F4[j][:, :], func=AF.Square,
                                                        accum_out=ST[:, 14:15]),
                     reads=["f4%d" % j], writes=["hnC", "ssM"])
                rstd_act(14, 15, 16, 1.0 / D, "M")
                S.op("act", lambda e, j=j: e.activation(out=hnC, in_=F4[j][:, :], func=AF.Copy,
                                                        scale=ST[:, 16:17]),
                     reads=["f4%d" % j, "rM"], writes=["hnC"])
                to_T(hnC, hnT2[:, :, j * 128:(j + 1) * 128], 8, "hnC", "hnT2")
            def up(fc):
                ub = 2 + fc % 2

                def f(e):
                    inst = None
                    for kc in range(8):
                        inst = e.matmul(banks[ub][:, 0:256], lhsT=WUP[:, kc, fc * 128:(fc + 1) * 128],
                                        rhs=hnT2[:, kc, :], start=(kc == 0), stop=(kc == 7))
                    return inst
                S.op("pe", f, reads=["hnT2", "wup"], writes=["b%d" % ub])
                rr = rC[fc % 2]
                S.op("act", lambda e: e.activation(out=rr, in_=banks[ub][:, 0:256], func=AF.Relu),
                     reads=[], writes=["b%d" % ub, "r%d" % (fc % 2)])
                S.op("dve", lambda e: e.tensor_tensor(out=actT[:, fc % 4, :], in0=rr, in1=rr, op=ALU.mult),
                     reads=["r%d" % (fc % 2)], writes=["act%d" % (fc % 4)])

            def down(fc):
                def f(e):
                    inst = None
                    for j in range(2):
                        for hf in range(2):
                            inst = e.matmul(banks[accb[j][hf]][:, :], lhsT=actT[:, fc % 4, j * 128:(j + 1) * 128],
                                            rhs=WDN[:, fc, hf * 512:(hf + 1) * 512],
                                            start=(fc == 0), stop=(fc == 31))
                    return inst
                S.op("pe", f, reads=["act%d" % (fc % 4), "wdn"], writes=["b4", "b5", "b6", "b7"])

            for i in range(34):
                if i < 32:
                    up(i)
                if 0 <= i - 2 < 32:
                    down(i - 2)
            for j in range(2):
                norm_residual(accb[j], F4[j], "f4%d" % j, "mlp", "d")
            for j in range(2):
                mblk = 2 * t + j
                S.op("act", lambda e, j=j: e.activation(out=hnC, in_=F4[j][:, :], func=AF.Copy),
                     reads=["f4%d" % j], writes=["hnC"])
                to_T(hnC, hnT2[:, :, j * 128:(j + 1) * 128], 8, "hnC", "hnT2")
                S.dma("sp", "pC", [(pC, p_own[mblk * 128:(mblk + 1) * 128, :])], writes=["pC"])
                S.op("dve", lambda e: e.tensor_copy(out=pbC, in_=pC), reads=["pC"], writes=["pbC"])
                to_T(pbC, pTC, 2, "pbC", "pTC")

                def f(e, j=j):
                    inst = None
                    for hf in range(2):
                        for kc in range(2):
                            inst = e.matmul(banks[2 + hf][:, :], lhsT=pTC[:, kc, :],
                                            rhs=WPLE[:, kc, hf * 512:(hf + 1) * 512],
                                            start=(kc == 0), stop=(kc == 1))
                    return inst
                S.op("pe", f, reads=["pTC", "ysb", "hn"], writes=["b2", "b3"])
                norm_residual([2, 3], None, None, "ple", "p")

                def fg(e, j=j):
                    inst = None
                    for hf in range(2):
                        for kc in range(8):
                            inst = e.matmul(banks[accb[j][hf]][:, :], lhsT=hnT2[:, kc, j * 128:(j + 1) * 128],
                                            rhs=WGT[:, kc, hf * 512:(hf + 1) * 512],
                                            start=(kc == 0), stop=(kc == 7))
                    return inst
                S.op("pe", fg, reads=["hnT2", "win"], writes=["b%d" % accb[j][0], "b%d" % accb[j][1]])
                cur_gain[0] = None
                for hf in range(2):
                    S.op("act", lambda e, j=j, hf=hf: e.activation(
                        out=gainC[:, hf * 512:(hf + 1) * 512], in_=banks[accb[j][hf]][:, :], func=AF.Sigmoid),
                        reads=[], writes=["b%d" % accb[j][hf], "gain"])
                S.op("dve", lambda e: e.tensor_tensor(out=tmpC, in0=tmpC, in1=gainC, op=ALU.mult),
                     reads=["tmpC", "gain"], writes=["tmpC"])
                S.op("dve", lambda e, j=j: e.tensor_tensor(out=F4[j][:, :], in0=F4[j][:, :], in1=tmpC, op=ALU.add),
                     reads=["tmpC", "f4%d" % j], writes=["f4%d" % j])
                outs.append(S.dma("sp", "yout%d" % j, [(y_out[mblk * 128:(mblk + 1) * 128, :], F4[j][:, :])],
                                  reads=["f4%d" % j]))
        S.finish(outs)
        S.emit(nc, st)
    return nc


def _consts():
    ident = np.eye(128, dtype=np.float32)
    p = np.arange(128)[:, None]
    i = np.arange(128)[None, :]
    negmask = np.where(i <= p, NEG, 0.0).astype(np.float32)
    blockones = np.where((p // 64) == (i // 64), 1.0 / 64.0, 0.0).astype(np.float32)
    return np.concatenate([ident, negmask, blockones, -ident], axis=1)


def _rc16(g):
    rc = np.zeros((4, 16), np.float32)
    for gi in range(4):
        w = 2 ** (gi + 1)
        for q in range(16):
            i = 112 + q
            if g == 0:
                cnt = min(128 - i, w)
            else:
                cnt = w
            rc[gi, q] = 1.0 / cnt
    return np.ascontiguousarray(np.broadcast_to(rc.reshape(1, 64), (128, 64)))


_PROG = {}


def _prepare(x, p, g_mix_pre, w_in, w_pool, pool_scale, g_sb, w_out, g_mix_post,
             g_mlp_pre, w_up, w_down, g_mlp_post, w_ple_gate, w_ple_proj, g_ple):
    f = np.float32
    x = np.asarray(x, f)
    p = np.asarray(p, f)

    def cols(v):
        return np.asarray(v, f).reshape(8, 128).T

    gcat = np.concatenate([np.asarray(pool_scale, f)[0], np.asarray(g_sb, f)[0]])
    gcols = np.ascontiguousarray(np.concatenate(
        [cols(np.asarray(g_mix_pre, f)[0]), cols(gcat), cols(np.asarray(g_mlp_pre, f)[0])], axis=1))
    g_post = np.ascontiguousarray(np.stack([np.asarray(g_mix_post, f)[0], np.asarray(g_mlp_post, f)[0],
                                            np.asarray(g_ple, f)[0]]))
    wpool = np.ascontiguousarray(np.asarray(w_pool, f)[0].transpose(1, 0, 2).reshape(128, 512))
    shared = {
        "w_in": np.ascontiguousarray(np.asarray(w_in, f)[0]),
        "w_pool": wpool,
        "w_out": np.ascontiguousarray(np.asarray(w_out, f)[0]),
        "w_up": np.ascontiguousarray(np.asarray(w_up, f)[0]),
        "w_down": np.ascontiguousarray(np.asarray(w_down, f)[0]),
        "w_gate": np.ascontiguousarray(np.asarray(w_ple_gate, f)[0]),
        "w_ple": np.ascontiguousarray(np.asarray(w_ple_proj, f)[0]),
        "gcols": gcols,
        "g_post": g_post,
        "consts": _consts(),
    }
    in_maps = []
    tok_maps = []
    for c in range(8):
        b, g = c // 4, c % 4
        t0 = (61 + g) * 128 - 1
        xk = np.zeros((S_LEN, D), f)
        n = t0 + 1
        xk[:n] = x[b, t0::-1][:n]
        toks = np.empty(2048, np.int64)
        for m in range(16):
            qb = 60 + g - 4 * m
            toks[m * 128:(m + 1) * 128] = qb * 128 + 127 - np.arange(128)
        tok_maps.append((b, toks))
        d = dict(shared)
        d["xk"] = xk
        d["p_own"] = np.ascontiguousarray(p[0, b][toks])
        d["rc16"] = _rc16(g)
        in_maps.append(d)
    return in_maps, tok_maps


def kernel(**inputs):
    if "nc" not in _PROG:
        _PROG["nc"] = build_program()
    nc = _PROG["nc"]
    in_maps, tok_maps = _prepare(**inputs)
    res = run_bass_kernel_spmd(nc, in_maps, core_ids=list(range(8)))
    f = np.float32
    out = np.empty((2, S_LEN, D), f)
    for c in range(8):
        b, toks = tok_maps[c]
        out[b, toks] = res.results[c]["y_out"]
    return out
```

```python
import contextlib
import numpy as np
import concourse.bass as bass
import concourse.mybir as mybir
from concourse.bass_utils import run_bass_kernel_spmd

F32 = mybir.dt.float32
BF16 = mybir.dt.bfloat16
AF = mybir.ActivationFunctionType
ALU = mybir.AluOpType

D = 1024
S_LEN = 8192
NB = 64
DFF = 4096
PLE = 256
EPS = 1e-6
NEG = -30000.0
ENGS = ("pe", "act", "dve", "pool", "sp")
SYNC_SAME_ENGINE = True
SUB_ENG = "pool"


class _Op:
    __slots__ = ("eng", "fn", "deps", "signal", "ticket", "sem", "dma", "n_dma")

    def __init__(self, eng, fn, dma, n_dma):
        self.eng = eng
        self.fn = fn
        self.deps = []
        self.signal = False
        self.ticket = None
        self.sem = None
        self.dma = dma
        self.n_dma = n_dma


class Sched:
    def __init__(self):
        self.ops = {e: [] for e in ENGS}
        self.last_w = {}
        self.readers = {}
        self.dma_keys = []
        self.final_deps = []
        self._dsem = {}

    def op(self, eng, fn, reads=(), writes=(), dma=None, n_dma=1):
        o = _Op(eng, fn, dma, n_dma)
        if dma is not None and dma not in self.dma_keys:
            self.dma_keys.append(dma)
        deps = []
        for r in reads:
            w = self.last_w.get(r)
            if w is not None:
                deps.append((w, "raw"))
        for r in writes:
            w = self.last_w.get(r)
            if w is not None:
                deps.append((w, "waw"))
            for rd in self.readers.get(r, ()):
                deps.append((rd, "war"))
        for d, kind in deps:
            if d is o:
                continue
            if d.eng == eng and d.dma is None and dma is None and kind != "raw" and not SYNC_SAME_ENGINE:
                continue
            if d.eng == "pe" and eng == "pe" and d.dma is None and dma is None:
                continue
            o.deps.append(d)
            d.signal = True
        for r in reads:
            self.readers.setdefault(r, []).append(o)
        for r in writes:
            self.last_w[r] = o
            self.readers[r] = []
        self.ops[eng].append(o)
        return o

    def dma(self, eng, key, pairs, reads=(), writes=()):
        sched = self

        def fn(e, pairs=pairs, key=key):
            inst = None
            for (o_ap, i_ap) in pairs:
                inst = e.dma_start(out=o_ap, in_=i_ap)
                inst.then_inc(sched._dsem[key], 16)
            return inst

        return self.op(eng, fn, reads=reads, writes=writes, dma=key, n_dma=len(pairs))

    def finish(self, ops):
        for o in ops:
            o.signal = True
            self.final_deps.append(o)

    def emit(self, nc, stack):
        esem = {e: stack.enter_context(nc.semaphore("s_" + e)) for e in ENGS}
        dsem = {k: stack.enter_context(nc.semaphore("d_%d" % i))
                for i, k in enumerate(self.dma_keys)}
        self._dsem = dsem
        dcount = {k: 0 for k in self.dma_keys}
        for e in ENGS:
            c = 0
            for o in self.ops[e]:
                if o.dma is not None:
                    dcount[o.dma] += 16 * o.n_dma
                    o.ticket = dcount[o.dma]
                    o.sem = dsem[o.dma]
                elif o.signal:
                    c += 1
                    o.ticket = c
                    o.sem = esem[e]
        final_deps = self.final_deps
        ops = self.ops

        def collect(deps):
            need = {}
            for d in deps:
                key = d.sem.num
                if need.get(key, (None, 0))[1] < d.ticket:
                    need[key] = (d.sem, d.ticket)
            return need

        def run(eng_name, eng):
            known = {}
            for o in ops[eng_name]:
                for key, (sem, val) in collect(o.deps).items():
                    if known.get(key, 0) >= val:
                        continue
                    eng.wait_ge(sem, val)
                    known[key] = val
                inst = o.fn(eng)
                if o.dma is None and o.signal:
                    inst.then_inc(o.sem, 1)
            if eng_name == "sp":
                for key, (sem, val) in collect(final_deps).items():
                    eng.wait_ge(sem, val)

        with nc.Block() as block:
            @block.tensor
            def _(e):
                run("pe", e)

            @block.scalar
            def _(e):
                run("act", e)

            @block.vector
            def _(e):
                run("dve", e)

            @block.gpsimd
            def _(e):
                run("pool", e)

            @block.sync
            def _(e):
                run("sp", e)


def build_program(n_groups=16, do_c=True, debug=False):
    nc = bass.Bass("TRN2", target_bir_lowering=False)

    def din(name, shape, dt=F32):
        return nc.dram_tensor(name, shape, dt, kind="ExternalInput").ap()

    xk = din("xk", [S_LEN, D])
    p_own = din("p_own", [2048, PLE])
    w_in = din("w_in", [D, 2048])
    w_pool = din("w_pool", [128, 512])
    w_out = din("w_out", [D, D])
    w_up = din("w_up", [D, DFF])
    w_down = din("w_down", [DFF, D])
    w_gate = din("w_gate", [D, D])
    w_ple = din("w_ple", [PLE, D])
    gcols = din("gcols", [128, 24])
    g_post = din("g_post", [3, D])
    consts = din("consts", [128, 512])
    rc16 = din("rc16", [128, 64])
    y_out = nc.dram_tensor("y_out", [2048, D], F32, kind="ExternalOutput").ap()
    yscr = nc.dram_tensor("yscr", [16, 128, 1024], BF16,
                          kind=("ExternalOutput" if debug else "Internal")).ap()

    S = Sched()
    with contextlib.ExitStack() as st:
        def sb(name, shape, dt):
            return st.enter_context(nc.sbuf_tensor(name, shape, dt))

        BIG = sb("BIG", [128, 65536], BF16)
        WIN = sb("WIN", [128, 16384], BF16)
        F4 = [sb("F4a", [128, 1024], F32), sb("F4b", [128, 1024], F32)]
        HNT = sb("HNT", [128, 4096], BF16)
        GB = sb("GB", [128, 1548], F32)
        SBF = sb("SBF", [128, 1032], F32)
        ABF = sb("ABF", [128, 1560], BF16)
        ATS = sb("ATS", [128, 1024], BF16)
        QTB = sb("QTB", [128, 1024], BF16)
        YP1 = sb("YP1", [128, 512], BF16)
        UT = sb("UT", [128, 576], F32)
        T12 = sb("T12", [128, 288], F32)
        DT = sb("DT", [128, 512], BF16)
        YH = sb("YH", [128, 2048], BF16)
        CST = sb("CST", [128, 512], BF16)
        WPL = sb("WPL", [128, 512], BF16)
        RC = sb("RC", [128, 64], F32)
        GC = sb("GC", [128, 24], F32)
        ST = sb("ST", [128, 32], F32)
        ONE = sb("ONE", [128, 1], F32)
        banks = [st.enter_context(nc.psum_tensor("B%d" % i, [128, 512], F32)) for i in range(8)]

        KT = BIG[:, 0:32768].rearrange("p (a t) -> p a t", a=4)
        V = BIG[:, 32768:65536].rearrange("p (a c) -> p a c", a=64)
        WUP = BIG[:, 0:32768].rearrange("p (a c) -> p a c", a=8)
        WDN = BIG[:, 32768:65536].rearrange("p (a c) -> p a c", a=32)
        WINV = WIN[:, :].rearrange("p (a c) -> p a c", a=8)
        WOUT = WIN[:, 0:8192].rearrange("p (a c) -> p a c", a=8)
        WGT = WIN[:, 8192:16384].rearrange("p (a c) -> p a c", a=8)
        hnT = HNT[:, :].rearrange("p (a t) -> p a t", a=8)
        yh = HNT[:, 0:2048].rearrange("p (j c t) -> p j c t", j=2, c=8)
        hnT2 = HNT[:, 2048:4096].rearrange("p (a t) -> p a t", a=8)
        gb = [GB[:, k * 516:k * 516 + 513] for k in range(3)]
        sbf = [SBF[:, 0:513], SBF[:, 516:1029]]
        sh = [ABF[:, k * 520:k * 520 + 513] for k in range(3)]
        sq0 = ABF[:, 0:512]
        ats = [ATS[:, 0:512], ATS[:, 512:1024]]
        QT2 = [QTB[:, k * 512:(k + 1) * 512].rearrange("p (a t) -> p a t", a=4) for k in range(2)]
        uT = UT[:, :].rearrange("p (a t) -> p a t", a=4)
        tt = [T12[:, 0:144], T12[:, 144:288]]
        dT = DT[:, :].rearrange("p (a t) -> p a t", a=4)
        ypool = [YH[:, 0:512], YP1[:, :]]
        ysb = YH[:, 512:1024]
        sqT = DT[:, :]
        yT3 = YH[:, 0:1024].rearrange("p (a t) -> p a t", a=8)
        hn = YH[:, 1024:2048]
        WPLE = YH[:, :].rearrange("p (a c) -> p a c", a=2)
        ident = CST[:, 0:128]
        negmask = CST[:, 128:256]
        blockones = CST[:, 256:384]
        negident = CST[:, 384:512]
        wpool = WPL[:, :].rearrange("p (a d) -> p a d", a=4)
        rc = RC[:, :].rearrange("p (a t) -> p a t", a=4)
        Bbf = [b[:, :].bitcast(BF16) for b in banks]
        tmpC = GB[:, 0:1024]
        gainC = SBF[:, 0:1024]
        hnC = ABF[:, 0:1024]
        actT = ATS[:, :].rearrange("p (a t) -> p a t", a=4)
        pbC = QTB[:, 0:256]
        pTC = QTB[:, 256:512].rearrange("p (a t) -> p a t", a=2)
        rC = [UT[:, 0:256], UT[:, 256:512]]
        pC = T12[:, 0:256]

        S.dma("pool", "cst", [(CST[:, :], consts)], writes=["cst"])
        S.dma("pool", "wpl", [(WPL[:, :], w_pool)], writes=["wpool"])
        S.dma("sp", "rc", [(RC[:, :], rc16)], writes=["rc"])
        S.dma("sp", "gc", [(GC[:, :], gcols)], writes=["gc"])
        S.op("pool", lambda e: e.memset(ONE[:, :], 1.0), writes=["one"])
        S.op("pool", lambda e: e.memset(GB[:, 0:1], 1.0), writes=["gb0"])
        S.op("pool", lambda e: e.memset(GB[:, 516:517], 1.0), writes=["gb1"])
        S.op("pool", lambda e: e.memset(GB[:, 1032:1033], 1.0), writes=["gb2"])

        for kc in range(8):
            for hf in range(2):
                i = kc * 2 + hf
                buf = F4[i % 2]
                S.dma("sp", "f4%d" % (i % 2),
                      [(buf[:, :], w_in[kc * 128:(kc + 1) * 128, hf * 1024:(hf + 1) * 1024])],
                      writes=["f4%d" % (i % 2)])
                eng = "act" if i % 2 == 0 else "dve"
                if eng == "act":
                    S.op("act", lambda e, kc=kc, hf=hf, buf=buf: e.activation(
                        out=WINV[:, kc, hf * 1024:(hf + 1) * 1024], in_=buf[:, :], func=AF.Copy,
                        scale=GC[:, kc:kc + 1]),
                        reads=["f4%d" % (i % 2), "gc"], writes=["win"])
                else:
                    S.op("dve", lambda e, kc=kc, hf=hf, buf=buf: e.tensor_scalar(
                        out=WINV[:, kc, hf * 1024:(hf + 1) * 1024], in0=buf[:, :],
                        scalar1=GC[:, kc:kc + 1], scalar2=None, op0=ALU.mult),
                        reads=["f4%d" % (i % 2), "gc"], writes=["win"])

        I32 = mybir.dt.int32
        STI = ST[:, :].bitcast(I32)

        def rstd_thunks(ss_col, c0, scale, tag, iters):
            y, a_, t_, u_ = c0, c0 + 1, c0 + 2, c0 + 3
            col = lambda c: ST[:, c:c + 1]
            coli = lambda c: STI[:, c:c + 1]
            th = []
            th.append(lambda: S.op("dve", lambda e: e.tensor_scalar(
                out=col(a_), in0=col(ss_col), scalar1=scale, scalar2=EPS, op0=ALU.mult, op1=ALU.add),
                reads=["ss" + tag], writes=["a" + tag]))
            th.append(lambda: S.op("dve", lambda e: e.tensor_scalar(
                out=coli(y), in0=coli(a_), scalar1=1, scalar2=None, op0=ALU.logical_shift_right),
                reads=["a" + tag], writes=["r" + tag]))
            th.append(lambda: S.op("dve", lambda e: e.tensor_scalar(
                out=coli(y), in0=coli(y), scalar1=-1, scalar2=0x5f3759df, op0=ALU.mult, op1=ALU.add),
                reads=["r" + tag], writes=["r" + tag]))
            for _ in range(iters):
                th.append(lambda: S.op("dve", lambda e: e.scalar_tensor_tensor(
                    out=col(t_), in0=col(y), scalar=col(a_), in1=col(y), op0=ALU.mult, op1=ALU.mult),
                    reads=["r" + tag, "a" + tag], writes=["t" + tag]))
                th.append(lambda: S.op("dve", lambda e: e.scalar_tensor_tensor(
                    out=col(u_), in0=col(t_), scalar=-0.5, in1=col(y), op0=ALU.mult, op1=ALU.mult),
                    reads=["t" + tag, "r" + tag], writes=["u" + tag]))
                th.append(lambda: S.op("dve", lambda e: e.scalar_tensor_tensor(
                    out=col(y), in0=col(y), scalar=1.5, in1=col(u_), op0=ALU.mult, op1=ALU.add),
                    reads=["u" + tag, "r" + tag], writes=["r" + tag]))
            return th

        def rstd_act(ss_col, t_col, r_col, scale, tag):
            S.op("dve", lambda e: e.tensor_scalar(out=ST[:, t_col:t_col + 1], in0=ST[:, ss_col:ss_col + 1],
                                                  scalar1=scale, scalar2=EPS, op0=ALU.mult, op1=ALU.add),
                 reads=["ss" + tag], writes=["t" + tag])
            S.op("act", lambda e: e.activation(out=ST[:, t_col:t_col + 1], in_=ST[:, t_col:t_col + 1],
                                               func=AF.Sqrt),
                 reads=["t" + tag], writes=["t" + tag])
            S.op("dve", lambda e: e.reciprocal(out=ST[:, r_col:r_col + 1], in_=ST[:, t_col:t_col + 1]),
                 reads=["t" + tag], writes=["r" + tag])

        xcnt = [0]
        step_ctr = [0]

        junkA = SBF[:, 516:1028].bitcast(BF16)

        def proj_kv(m):
            out = []
            tag = "A"
            for bi in range(4):
                pos = 4 * m + bi
                xi = xcnt[0] % 2
                xcnt[0] += 1
                xt = F4[xi]
                xn = "f4%d" % xi
                out.append(lambda xt=xt, xn=xn, pos=pos: S.dma(
                    "sp", xn, [(xt[:, :], xk[pos * 128:(pos + 1) * 128, :])], writes=[xn]))
                out.append(lambda xt=xt, xn=xn, bi=bi: S.op(
                    "act", lambda e: e.activation(out=junkA, in_=xt[:, :], func=AF.Square,
                                                  accum_out=ST[:, bi:bi + 1]),
                    reads=[xn], writes=["junkA", "ssA%d" % bi]))
            out.append(lambda: S.op(
                "dve", lambda e: e.tensor_scalar(out=ST[:, 28:32], in0=ST[:, 0:4], scalar1=1.0 / D, scalar2=EPS,
                                                 op0=ALU.mult, op1=ALU.add),
                reads=["ssA0", "ssA1", "ssA2", "ssA3"], writes=["tA"]))
            out.append(lambda: S.op(
                "act", lambda e: e.activation(out=ST[:, 28:32], in_=ST[:, 28:32], func=AF.Sqrt),
                reads=["tA"], writes=["tA"]))
            out.append(lambda: S.op(
                "dve", lambda e: e.reciprocal(out=ST[:, 24:28], in_=ST[:, 28:32]),
                reads=["tA"], writes=["rA"]))
            per_block = []
            for bi in range(4):
                th = []
                pos = 4 * m + bi
                xi = xcnt[0] % 2
                xcnt[0] += 1
                xt = F4[xi]
                xn = "f4%d" % xi
                th.append(lambda xt=xt, xn=xn, pos=pos: S.dma(
                    "sp", xn, [(xt[:, :], xk[pos * 128:(pos + 1) * 128, :])], writes=[xn]))
                th.append(lambda xt=xt, xn=xn, bi=bi: S.op(
                    "act", lambda e: e.activation(out=hn, in_=xt[:, :], func=AF.Copy, scale=ST[:, 24 + bi:25 + bi]),
                    reads=[xn, "rA"], writes=["hn"]))

                for h2 in range(2):
                    def tr4(e, h2=h2):
                        inst = None
                        for kc in range(4 * h2, 4 * h2 + 4):
                            inst = e.transpose(Bbf[0][:, kc * 128:(kc + 1) * 128],
                                               hn[:, kc * 128:(kc + 1) * 128], ident)
                        return inst
                    th.append(lambda tr4=tr4: S.op("pe", tr4, reads=["hn", "cst"], writes=["b0"]))
                th.append(lambda bi=bi: S.op(
                    "dve", lambda e: e.tensor_copy(out=hnT[:, :, bi * 128:(bi + 1) * 128],
                                                   in_=Bbf[0][:, :].rearrange("p (a t) -> p a t", a=8)),
                    reads=[], writes=["b0", "hnT%d" % bi]))
                for q4 in range(4):
                    def vmm(e, bi=bi, q4=q4):
                        inst = None
                        for kc in range(2 * q4, 2 * q4 + 2):
                            inst = e.matmul(banks[1][:, :], lhsT=hnT[:, kc, bi * 128:(bi + 1) * 128],
                                            rhs=WINV[:, kc, 1536:2048], start=(kc == 0), stop=(kc == 7))
                        return inst
                    th.append(lambda vmm=vmm, bi=bi: S.op("pe", vmm, reads=["hnT%d" % bi, "win"], writes=["b1"]))
                th.append(lambda pos=pos, m=m: S.op(
                    "act", lambda e: e.activation(out=V[:, pos, :], in_=banks[1][:, :], func=AF.Copy),
                    reads=[], writes=["b1", "V%d" % m]))
                per_block.append(th)
            stagger = 6
            keyed = []
            for bi, th in enumerate(per_block):
                for k, t_ in enumerate(th):
                    keyed.append((k + bi * stagger, bi, k, t_))
            keyed.sort(key=lambda z: (z[0], z[1]))
            out += [z[3] for z in keyed]
            allT = ["hnT%d" % bi for bi in range(4)]
            for pr in range(4):
                for q4 in range(4):
                    def kmm(e, pr=pr, q4=q4):
                        inst = None
                        for kc in range(2 * q4, 2 * q4 + 2):
                            inst = e.matmul(banks[1][:, :],
                                            lhsT=WINV[:, kc, 1024 + pr * 128:1024 + (pr + 1) * 128],
                                            rhs=hnT[:, kc, :], start=(kc == 0), stop=(kc == 7))
                        return inst
                    out.append(lambda kmm=kmm: S.op("pe", kmm, reads=allT + ["win"], writes=["b1"]))
                out.append(lambda pr=pr, m=m: S.op(
                    "dve", lambda e: e.tensor_copy(out=KT[:, pr, m * 512:(m + 1) * 512], in_=banks[1][:, :]),
                    reads=[], writes=["b1", "KT%d" % m]))
            return out

        def proj_own(m):
            qp = m % 2
            th = []

            def t_qp(pr):
                def qmm(e):
                    inst = None
                    for kc in range(8):
                        inst = e.matmul(banks[1][:, pr * 128:(pr + 1) * 128],
                                        lhsT=WINV[:, kc, 512 + pr * 128:512 + (pr + 1) * 128],
                                        rhs=hnT[:, kc, 0:128], start=(kc == 0), stop=(kc == 7))
                    return inst
                S.op("pe", qmm, reads=["hnT0", "win"], writes=["b1"])
            for pr in range(4):
                th.append(lambda pr=pr: t_qp(pr))
            th.append(lambda: S.op(
                "dve", lambda e: e.tensor_copy(out=QTB[:, qp * 512:(qp + 1) * 512], in_=banks[1][:, :]),
                reads=[], writes=["b1", "QT%d" % qp]))

            def t_u(rd):
                def umm(e):
                    inst = None
                    for q in range(2):
                        gi = rd * 2 + q
                        for kc in range(8):
                            inst = e.matmul(banks[1][:, q * 144:(q + 1) * 144],
                                            lhsT=WINV[:, kc, gi * 128:(gi + 1) * 128],
                                            rhs=hnT[:, kc, 0:144], start=(kc == 0), stop=(kc == 7))
                    return inst
                S.op("pe", umm, reads=["hnT0", "hnT1", "win"], writes=["b1"])
                S.op("act", lambda e: e.activation(out=UT[:, rd * 288:(rd + 1) * 288],
                                                   in_=banks[1][:, 0:288], func=AF.Copy),
                     reads=[], writes=["b1", "uT"])
            th.append(lambda: t_u(0))
            th.append(lambda: t_u(1))

            def t_mix(gi):
                a_ = uT[:, gi, :]
                w = 2 ** (gi + 1)
                S.op("dve", lambda e: e.tensor_tensor(out=tt[0][:, 0:143], in0=a_[:, 0:143],
                                                      in1=a_[:, 1:144], op=ALU.add),
                     reads=["uT"], writes=["tt0"])
                cur, ln, sh_ = 0, 143, 2
                for lvl in range(gi):
                    nl = ln - sh_
                    S.op("dve", lambda e, cur=cur, nl=nl, sh_=sh_: e.tensor_tensor(
                        out=tt[1 - cur][:, 0:nl], in0=tt[cur][:, 0:nl], in1=tt[cur][:, sh_:sh_ + nl], op=ALU.add),
                        reads=["tt%d" % cur], writes=["tt%d" % (1 - cur)])
                    cur, ln, sh_ = 1 - cur, nl, sh_ * 2
                S.op("dve", lambda e, cur=cur: e.scalar_tensor_tensor(
                    out=dT[:, gi, :], in0=tt[cur][:, 0:128], scalar=1.0 / w, in1=a_[:, 0:128],
                    op0=ALU.mult, op1=ALU.subtract),
                    reads=["tt%d" % cur, "uT"], writes=["dT"])
                if m == 15:
                    S.op("dve", lambda e, cur=cur: e.tensor_tensor(
                        out=tt[1 - cur][:, 0:16], in0=tt[cur][:, 112:128], in1=rc[:, gi, :], op=ALU.mult),
                        reads=["tt%d" % cur, "rc"], writes=["tt%d" % (1 - cur)])
                    S.op("dve", lambda e, cur=cur: e.tensor_tensor(
                        out=dT[:, gi, 112:128], in0=tt[1 - cur][:, 0:16], in1=a_[:, 112:128], op=ALU.subtract),
                        reads=["tt%d" % (1 - cur), "uT"], writes=["dT"])
            for gi in range(4):
                th.append(lambda gi=gi: t_mix(gi))

            def t_p():
                def pmm(e):
                    inst = None
                    for gi in range(4):
                        inst = e.matmul(banks[1][:, gi * 128:(gi + 1) * 128], lhsT=wpool[:, gi, :],
                                        rhs=dT[:, gi, :], start=True, stop=True)
                    return inst
                S.op("pe", pmm, reads=["dT", "wpool"], writes=["b1"])
                S.op("act", lambda e: e.activation(out=ypool[qp], in_=banks[1][:, :], func=AF.Copy),
                     reads=[], writes=["b1", "yp%d" % qp])
            th.append(t_p)
            return th

        def attention(m, bg):
            sp_ = (15 - m) % 2
            oT = banks[6 + sp_]
            oTn = "oT%d" % sp_
            nch = 16 - m
            steps = []
            for pr in range(4):
                for c in range(nch):
                    for hh in range(2):
                        steps.append((pr, hh, c))
            K = len(steps)
            base = step_ctr[0]
            step_ctr[0] += K
            QT = QT2[m % 2]
            QTn = "QT%d" % (m % 2)

            def emit_qk(i):
                pr, hh, c = steps[i]
                par = (base + i) % 2
                g3 = (base + i) % 3
                lo, hi = hh * 64, hh * 64 + 64
                k0 = (4 * m + 4 * c) * 128

                def f(e):
                    inst = e.matmul(banks[2 + par][:, :], lhsT=QT[lo:hi, pr, :], rhs=KT[lo:hi, pr, k0:k0 + 512],
                                    start=True, stop=(c != 0))
                    if c == 0:
                        inst = e.matmul(banks[2 + par][:, 0:128], lhsT=ident, rhs=negmask,
                                        start=False, stop=True)
                    return inst
                S.op("pe", f, reads=[QTn, "KT%d" % (m + c), "cst"], writes=["z%d" % par])
                S.op("act", lambda e: e.activation(out=gb[g3][:, 1:513], in_=banks[2 + par][:, :],
                                                   func=AF.Sigmoid, scale=-0.125),
                     reads=[], writes=["z%d" % par, "gb%d" % g3])

            def emit_scan(i):
                pr, hh, c = steps[i]
                g3 = (base + i) % 3
                if c == 0:
                    init = 1.0
                    rds = ["gb%d" % g3, "one"]
                else:
                    p3 = (base + i - 2) % 3
                    init = sh[p3][:, 512:513]
                    rds = ["gb%d" % g3, "one", "sh%d" % p3]
                S.op("dve", lambda e: e.tensor_tensor_scan(
                    out=sh[g3], data0=gb[g3], data1=ONE[:, 0:1].to_broadcast([128, 513]),
                    initial=init, op0=ALU.mult, op1=ALU.mult),
                    reads=rds, writes=["sh%d" % g3])

            def emit_tr(i):
                par = (base + i) % 2
                s3 = (base + i) % 3

                def f(e):
                    inst = None
                    for j in range(4):
                        e.matmul(banks[4 + par][:, j * 128:(j + 1) * 128], lhsT=sh[s3][:, j * 128:j * 128 + 128],
                                 rhs=ident, start=True, stop=False)
                        inst = e.matmul(banks[4 + par][:, j * 128:(j + 1) * 128],
                                        lhsT=sh[s3][:, j * 128 + 1:j * 128 + 129],
                                        rhs=negident, start=False, stop=True)
                    return inst
                S.op("pe", f, reads=["sh%d" % s3, "cst"], writes=["atp%d" % par])
                S.op("act", lambda e: e.activation(out=ats[par], in_=banks[4 + par][:, :], func=AF.Copy),
                     reads=[], writes=["atp%d" % par, "ats%d" % par])

            def emit_av(i):
                pr, hh, c = steps[i]
                par = (base + i) % 2
                head = 2 * pr + hh
                lo, hi = hh * 64, hh * 64 + 64

                def f(e):
                    inst = None
                    for j in range(4):
                        pos = 4 * m + 4 * c + j
                        inst = e.matmul(oT[lo:hi, pr * 128:(pr + 1) * 128],
                                        lhsT=V[:, pos, head * 64:(head + 1) * 64],
                                        rhs=ats[par][:, j * 128:(j + 1) * 128],
                                        start=(c == 0 and j == 0), stop=(c == nch - 1 and j == 3))
                    return inst
                S.op("pe", f, reads=["ats%d" % par, "V%d" % (m + c)], writes=[oTn])

            n_it = K + 5
            done = 0
            for it, i in enumerate(range(-3, K + 2)):
                if 0 <= i < K:
                    emit_scan(i)
                if 0 <= i + 3 < K:
                    emit_qk(i + 3)
                if 0 <= i - 1 < K:
                    emit_tr(i - 1)
                if 0 <= i - 2 < K:
                    emit_av(i - 2)
                tgt = (len(bg) * (it + 1)) // n_it
                while done < tgt:
                    bg[done]()
                    done += 1
            while done < len(bg):
                bg[done]()
                done += 1

        def tail_thunks(m):
            sp_ = (15 - m) % 2
            oT = banks[6 + sp_]
            oTn = "oT%d" % sp_
            qp = m % 2
            th = []
            th.append(lambda: S.op("act", lambda e: e.activation(out=sqT, in_=oT[:, :], func=AF.Square),
                                   reads=[], writes=[oTn, "dT"]))
            th.append(lambda: S.op("pe", lambda e: e.matmul(banks[1][:, :], lhsT=blockones, rhs=sqT,
                                                            start=True, stop=True),
                                   reads=["dT", "cst"], writes=["b1"]))
            th.append(lambda: S.op("dve", lambda e: e.tensor_scalar(out=sbf[0][:, 0:512], in0=banks[1][:, :],
                                                                    scalar1=EPS, scalar2=None, op0=ALU.add),
                                   reads=[], writes=["b1", "sb0"]))
            th.append(lambda: S.op("act", lambda e: e.activation(out=sbf[0][:, 0:512], in_=sbf[0][:, 0:512],
                                                                 func=AF.Sqrt),
                                   reads=["sb0"], writes=["sb0"]))
            th.append(lambda: S.op("dve", lambda e: e.reciprocal(out=sbf[0][:, 0:512], in_=sbf[0][:, 0:512]),
                                   reads=["sb0"], writes=["sb0"]))
            th.append(lambda: S.op("dve", lambda e: e.tensor_tensor(out=ysb, in0=oT[:, :],
                                                                    in1=sbf[0][:, 0:512], op=ALU.mult),
                                   reads=["sb0"], writes=[oTn, "ysb"]))
            th.append(lambda: S.dma("sp", "yscr", [(yscr[m][:, 0:512], ypool[qp]), (yscr[m][:, 512:1024], ysb)],
                                    reads=["yp%d" % qp, "ysb"], writes=["yscr%d" % m]))
            return th

        stg = [0]

        def stage(dst_view, src_ap, gcol, wname, extra=()):
            i = stg[0]
            stg[0] += 1
            buf = F4[i % 2]
            bn = "f4%d" % (i % 2)
            S.dma("sp", bn, [(buf[:, :], src_ap)], writes=[bn])
            if i % 2 == 0:
                S.op("act", lambda e: e.activation(out=dst_view, in_=buf[:, :], func=AF.Copy,
                                                   scale=GC[:, gcol:gcol + 1]),
                     reads=[bn, "gc"], writes=[wname] + list(extra))
            else:
                S.op("dve", lambda e: e.tensor_scalar(out=dst_view, in0=buf[:, :], scalar1=GC[:, gcol:gcol + 1],
                                                      scalar2=None, op0=ALU.mult),
                     reads=[bn, "gc"], writes=[wname] + list(extra))

        def win_loads():
            th = []
            for kc in range(8):
                th.append(lambda kc=kc: stage(WOUT[:, kc, :], w_out[kc * 128:(kc + 1) * 128, :], 8 + kc, "win"))
            th.append(lambda: S.dma(
                "pool", "wgt", [(WGT[:, kc * 4:(kc + 1) * 4, :],
                                 w_gate[kc * 512:(kc + 1) * 512, :].rearrange("(c p) n -> p c n", p=128))
                                for kc in range(2)], writes=["win"]))
            return th

        m_lo = 16 - n_groups
        for t_ in proj_kv(15) + proj_own(15):
            t_()
        win_done = False
        prev_tail = []
        for m_ in range(15, m_lo - 1, -1):
            bg = list(prev_tail)
            if m_ - 1 >= m_lo:
                bg += proj_kv(m_ - 1) + proj_own(m_ - 1)
            elif do_c:
                bg += win_loads()
                win_done = True
            attention(m_, bg)
            prev_tail = tail_thunks(m_)
        for t_ in prev_tail:
            t_()

        if not do_c:
            o = S.dma("sp", "yout", [(y_out[0:128, 0:512], UT[:, 0:512])], reads=["uT"])
            S.finish([o])
            S.emit(nc, st)
            return nc

        allKV = ["KT%d" % i for i in range(16)] + ["V%d" % i for i in range(16)]
        if not win_done:
            for t_ in win_loads():
                t_()
        S.dma("pool", "wple", [(WPLE[:, :, :], w_ple.rearrange("(c p) n -> p c n", p=128))],
              reads=[], writes=["yp0", "ysb", "hn"])
        npiece = 0
        for q in range(4):
            for kc in range(8):
                stage(WUP[:, kc, q * 1024:(q + 1) * 1024],
                      w_up[kc * 128:(kc + 1) * 128, q * 1024:(q + 1) * 1024], 16 + kc, "wup%d" % q,
                      extra=(allKV if npiece < 2 else ()))
                npiece += 1
        for c4 in range(8):
            S.dma("pool", "wdn%d" % c4,
                  [(WDN[:, c4 * 4:(c4 + 1) * 4, :],
                    w_down[c4 * 512:(c4 + 1) * 512, :].rearrange("(c p) n -> p c n", p=128))],
                  writes=(allKV if c4 == 0 else []) + ["wdn%d" % c4])

        outs = []
        gidx = {"mix": 0, "mlp": 1, "ple": 2}

        cur_gain = [None]

        def load_gain(which):
            if cur_gain[0] == which:
                return
            cur_gain[0] = which
            S.dma("sp", "gain", [(gainC, g_post[gidx[which]:gidx[which] + 1, :].partition_broadcast(128))],
                  writes=["gain"])

        def norm_residual(src_banks, xt, xn, which, tagc):
            for hf in range(2):
                S.op("act", lambda e, hf=hf: e.activation(out=hnC[:, hf * 512:(hf + 1) * 512],
                                                          in_=banks[src_banks[hf]][:, :], func=AF.Square,
                                                          accum_out=ST[:, 4 + hf:5 + hf]),
                     reads=[], writes=["b%d" % src_banks[hf], "hnC", "ssq%d" % hf])
            S.op("dve", lambda e: e.tensor_tensor(out=ST[:, 6:7], in0=ST[:, 4:5], in1=ST[:, 5:6], op=ALU.add),
                 reads=["ssq0", "ssq1"], writes=["ssN"])
            rstd_act(6, 7, 8, 1.0 / D, "N")
            load_gain(which)
            for hf in range(2):
                S.op("dve", lambda e, hf=hf: e.scalar_tensor_tensor(
                    out=tmpC[:, hf * 512:(hf + 1) * 512], in0=banks[src_banks[hf]][:, :], scalar=ST[:, 8:9],
                    in1=gainC[:, hf * 512:(hf + 1) * 512], op0=ALU.mult, op1=ALU.mult),
                    reads=["rN", "gain"], writes=["b%d" % src_banks[hf], "tmpC"])
            if xt is not None:
                S.op("dve", lambda e: e.tensor_tensor(out=xt[:, :], in0=xt[:, :], in1=tmpC, op=ALU.add),
                     reads=["tmpC", xn], writes=[xn])

        def to_T(src_ap, dst3, nk, src_name, dst_name):
            def f(e):
                inst = None
                for kc in range(nk):
                    inst = e.transpose(Bbf[0][:, kc * 128:(kc + 1) * 128], src_ap[:, kc * 128:(kc + 1) * 128], ident)
                return inst
            S.op("pe", f, reads=[src_name, "cst"], writes=["b0"])
            S.op("dve", lambda e: e.tensor_copy(
                out=dst3, in_=Bbf[0][:, 0:nk * 128].rearrange("p (a t) -> p a t", a=nk)),
                reads=[], writes=["b0", dst_name])

        for t in range(8):
            accb = [[4, 5], [6, 7]]
            for j in range(2):
                mblk = 2 * t + j
                S.dma("sp", "yh%d" % j, [(HNT[:, j * 1024:(j + 1) * 1024], yscr[mblk])],
                      reads=["yscr%d" % mblk], writes=["yh%d" % j])
                S.dma("sp", "f4%d" % j, [(F4[j][:, :], xk[(4 * mblk) * 128:(4 * mblk + 1) * 128, :])],
                      writes=["f4%d" % j])
            for j in range(2):
                def f(e, j=j):
                    inst = None
                    for hf in range(2):
                        for kc in range(8):
                            inst = e.matmul(banks[accb[j][hf]][:, :], lhsT=yh[:, j, kc, :],
                                            rhs=WOUT[:, kc, hf * 512:(hf + 1) * 512],
                                            start=(kc == 0), stop=(kc == 7))
                    return inst
                S.op("pe", f, reads=["yh%d" % j, "win"], writes=["b%d" % accb[j][0], "b%d" % accb[j][1]])
                norm_residual(accb[j], F4[j], "f4%d" % j, "mix", "m")
            for j in range(2):
                S.op("act", lambda e, j=j: e.activation(out=hnC, in_=F4[j][:, :], func=AF.Square,
                                                        accum_out=ST[:, 14:15]),
                     reads=["f4%d" % j], writes=["hnC", "ssM"])
                rstd_act(14, 15, 16, 1.0 / D, "M")
                S.op("act", lambda e, j=j: e.activation(out=hnC, in_=F4[j][:, :], func=AF.Copy,
                                                        scale=ST[:, 16:17]),
                     reads=["f4%d" % j, "rM"], writes=["hnC"])
                to_T(hnC, hnT2[:, :, j * 128:(j + 1) * 128], 8, "hnC", "hnT2")
            def up(fc):
                ub = 2 + fc % 2

                def f(e):
                    inst = None
                    for kc in range(8):
                        inst = e.matmul(banks[ub][:, 0:256], lhsT=WUP[:, kc, fc * 128:(fc + 1) * 128],
                                        rhs=hnT2[:, kc, :], start=(kc == 0), stop=(kc == 7))
                    return inst
                S.op("pe", f, reads=["hnT2", "wup%d" % (fc // 8)], writes=["b%d" % ub])
                rr = rC[fc % 2]
                S.op("act", lambda e: e.activation(out=rr, in_=banks[ub][:, 0:256], func=AF.Relu),
                     reads=[], writes=["b%d" % ub, "r%d" % (fc % 2)])
                S.op("dve", lambda e: e.tensor_tensor(out=actT[:, fc % 4, :], in0=rr, in1=rr, op=ALU.mult),
                     reads=["r%d" % (fc % 2)], writes=["act%d" % (fc % 4)])

            def down(fc):
                def f(e):
                    inst = None
                    for j in range(2):
                        for hf in range(2):
                            inst = e.matmul(banks[accb[j][hf]][:, :], lhsT=actT[:, fc % 4, j * 128:(j + 1) * 128],
                                            rhs=WDN[:, fc, hf * 512:(hf + 1) * 512],
                                            start=(fc == 0), stop=(fc == 31))
                    return inst
                S.op("pe", f, reads=["act%d" % (fc % 4), "wdn%d" % (fc // 4)], writes=["b4", "b5", "b6", "b7"])

            for i in range(34):
                if i < 32:
                    up(i)
                if 0 <= i - 2 < 32:
                    down(i - 2)
            for j in range(2):
                norm_residual(accb[j], F4[j], "f4%d" % j, "mlp", "d")
            for j in range(2):
                mblk = 2 * t + j
                S.op("act", lambda e, j=j: e.activation(out=hnC, in_=F4[j][:, :], func=AF.Copy),
                     reads=["f4%d" % j], writes=["hnC"])
                to_T(hnC, hnT2[:, :, j * 128:(j + 1) * 128], 8, "hnC", "hnT2")
                S.dma("sp", "pC", [(pC, p_own[mblk * 128:(mblk + 1) * 128, :])], writes=["pC"])
                S.op("dve", lambda e: e.tensor_copy(out=pbC, in_=pC), reads=["pC"], writes=["pbC"])
                to_T(pbC, pTC, 2, "pbC", "pTC")

                def f(e, j=j):
                    inst = None
                    for hf in range(2):
                        for kc in range(2):
                            inst = e.matmul(banks[2 + hf][:, :], lhsT=pTC[:, kc, :],
                                            rhs=WPLE[:, kc, hf * 512:(hf + 1) * 512],
                                            start=(kc == 0), stop=(kc == 1))
                    return inst
                S.op("pe", f, reads=["pTC", "ysb", "hn"], writes=["b2", "b3"])
                norm_residual([2, 3], None, None, "ple", "p")

                def fg(e, j=j):
                    inst = None
                    for hf in range(2):
                        for kc in range(8):
                            inst = e.matmul(banks[accb[j][hf]][:, :], lhsT=hnT2[:, kc, j * 128:(j + 1) * 128],
                                            rhs=WGT[:, kc, hf * 512:(hf + 1) * 512],
                                            start=(kc == 0), stop=(kc == 7))
                    return inst
                S.op("pe", fg, reads=["hnT2", "win"], writes=["b%d" % accb[j][0], "b%d" % accb[j][1]])
                cur_gain[0] = None
                for hf in range(2):
                    S.op("act", lambda e, j=j, hf=hf: e.activation(
                        out=gainC[:, hf * 512:(hf + 1) * 512], in_=banks[accb[j][hf]][:, :], func=AF.Sigmoid),
                        reads=[], writes=["b%d" % accb[j][hf], "gain"])
                S.op("dve", lambda e: e.tensor_tensor(out=tmpC, in0=tmpC, in1=gainC, op=ALU.mult),
                     reads=["tmpC", "gain"], writes=["tmpC"])
                S.op("dve", lambda e, j=j: e.tensor_tensor(out=F4[j][:, :], in0=F4[j][:, :], in1=tmpC, op=ALU.add),
                     reads=["tmpC", "f4%d" % j], writes=["f4%d" % j])
                outs.append(S.dma("sp", "yout%d" % j, [(y_out[mblk * 128:(mblk + 1) * 128, :], F4[j][:, :])],
                                  reads=["f4%d" % j]))
        S.finish(outs)
        S.emit(nc, st)
    return nc


def _consts():
    ident = np.eye(128, dtype=np.float32)
    p = np.arange(128)[:, None]
    i = np.arange(128)[None, :]
    negmask = np.where(i <= p, NEG, 0.0).astype(np.float32)
    blockones = np.where((p // 64) == (i // 64), 1.0 / 64.0, 0.0).astype(np.float32)
    return np.concatenate([ident, negmask, blockones, -ident], axis=1)


def _rc16(g):
    rc = np.zeros((4, 16), np.float32)
    for gi in range(4):
        w = 2 ** (gi + 1)
        for q in range(16):
            i = 112 + q
            if g == 0:
                cnt = min(128 - i, w)
            else:
                cnt = w
            rc[gi, q] = 1.0 / cnt
    return np.ascontiguousarray(np.broadcast_to(rc.reshape(1, 64), (128, 64)))


_PROG = {}


def _prepare(x, p, g_mix_pre, w_in, w_pool, pool_scale, g_sb, w_out, g_mix_post,
             g_mlp_pre, w_up, w_down, g_mlp_post, w_ple_gate, w_ple_proj, g_ple):
    f = np.float32
    x = np.asarray(x, f)
    p = np.asarray(p, f)

    def cols(v):
        return np.asarray(v, f).reshape(8, 128).T

    gcat = np.concatenate([np.asarray(pool_scale, f)[0], np.asarray(g_sb, f)[0]])
    gcols = np.ascontiguousarray(np.concatenate(
        [cols(np.asarray(g_mix_pre, f)[0]), cols(gcat), cols(np.asarray(g_mlp_pre, f)[0])], axis=1))
    g_post = np.ascontiguousarray(np.stack([np.asarray(g_mix_post, f)[0], np.asarray(g_mlp_post, f)[0],
                                            np.asarray(g_ple, f)[0]]))
    wpool = np.ascontiguousarray(np.asarray(w_pool, f)[0].transpose(1, 0, 2).reshape(128, 512))
    shared = {
        "w_in": np.ascontiguousarray(np.asarray(w_in, f)[0]),
        "w_pool": wpool,
        "w_out": np.ascontiguousarray(np.asarray(w_out, f)[0]),
        "w_up": np.ascontiguousarray(np.asarray(w_up, f)[0]),
        "w_down": np.ascontiguousarray(np.asarray(w_down, f)[0]),
        "w_gate": np.ascontiguousarray(np.asarray(w_ple_gate, f)[0]),
        "w_ple": np.ascontiguousarray(np.asarray(w_ple_proj, f)[0]),
        "gcols": gcols,
        "g_post": g_post,
        "consts": _consts(),
    }
    in_maps = []
    tok_maps = []
    for c in range(8):
        b, g = c // 4, c % 4
        t0 = (61 + g) * 128 - 1
        xk = np.zeros((S_LEN, D), f)
        n = t0 + 1
        xk[:n] = x[b, t0::-1][:n]
        toks = np.empty(2048, np.int64)
        for m in range(16):
            qb = 60 + g - 4 * m
            toks[m * 128:(m + 1) * 128] = qb * 128 + 127 - np.arange(128)
        tok_maps.append((b, toks))
        d = dict(shared)
        d["xk"] = xk
        d["p_own"] = np.ascontiguousarray(p[0, b][toks])
        d["rc16"] = _rc16(g)
        in_maps.append(d)
    return in_maps, tok_maps


def kernel(**inputs):
    if "nc" not in _PROG:
        _PROG["nc"] = build_program()
    nc = _PROG["nc"]
    in_maps, tok_maps = _prepare(**inputs)
    res = run_bass_kernel_spmd(nc, in_maps, core_ids=list(range(8)))
    f = np.float32
    out = np.empty((2, S_LEN, D), f)
    for c in range(8):
        b, toks = tok_maps[c]
        out[b, toks] = res.results[c]["y_out"]
    return out
```

```python
import contextlib
import numpy as np
import concourse.bass as bass
import concourse.mybir as mybir
from concourse.bass_utils import run_bass_kernel_spmd

F32 = mybir.dt.float32
BF16 = mybir.dt.bfloat16
AF = mybir.ActivationFunctionType
ALU = mybir.AluOpType

D = 1024
S_LEN = 8192
NB = 64
DFF = 4096
PLE = 256
EPS = 1e-6
NEG = -30000.0
ENGS = ("pe", "act", "dve", "pool", "sp")
SYNC_SAME_ENGINE = True
SUB_ENG = "pool"


class _Op:
    __slots__ = ("eng", "fn", "deps", "signal", "ticket", "sem", "dma", "n_dma")

    def __init__(self, eng, fn, dma, n_dma):
        self.eng = eng
        self.fn = fn
        self.deps = []
        self.signal = False
        self.ticket = None
        self.sem = None
        self.dma = dma
        self.n_dma = n_dma


class Sched:
    def __init__(self):
        self.ops = {e: [] for e in ENGS}
        self.last_w = {}
        self.readers = {}
        self.dma_keys = []
        self.final_deps = []
        self._dsem = {}

    def op(self, eng, fn, reads=(), writes=(), dma=None, n_dma=1):
        o = _Op(eng, fn, dma, n_dma)
        if dma is not None and dma not in self.dma_keys:
            self.dma_keys.append(dma)
        deps = []
        for r in reads:
            w = self.last_w.get(r)
            if w is not None:
                deps.append((w, "raw"))
        for r in writes:
            w = self.last_w.get(r)
            if w is not None:
                deps.append((w, "waw"))
            for rd in self.readers.get(r, ()):
                deps.append((rd, "war"))
        for d, kind in deps:
            if d is o:
                continue
            if d.eng == eng and d.dma is None and dma is None and kind != "raw" and not SYNC_SAME_ENGINE:
                continue
            if d.eng == "pe" and eng == "pe" and d.dma is None and dma is None:
                continue
            o.deps.append(d)
            d.signal = True
        for r in reads:
            self.readers.setdefault(r, []).append(o)
        for r in writes:
            self.last_w[r] = o
            self.readers[r] = []
        self.ops[eng].append(o)
        return o

    def dma(self, eng, key, pairs, reads=(), writes=()):
        sched = self

        def fn(e, pairs=pairs, key=key):
            inst = None
            for (o_ap, i_ap) in pairs:
                inst = e.dma_start(out=o_ap, in_=i_ap)
                inst.then_inc(sched._dsem[key], 16)
            return inst

        return self.op(eng, fn, reads=reads, writes=writes, dma=key, n_dma=len(pairs))

    def finish(self, ops):
        for o in ops:
            o.signal = True
            self.final_deps.append(o)

    def emit(self, nc, stack):
        esem = {e: stack.enter_context(nc.semaphore("s_" + e)) for e in ENGS}
        dsem = {k: stack.enter_context(nc.semaphore("d_%d" % i))
                for i, k in enumerate(self.dma_keys)}
        self._dsem = dsem
        dcount = {k: 0 for k in self.dma_keys}
        for e in ENGS:
            c = 0
            for o in self.ops[e]:
                if o.dma is not None:
                    dcount[o.dma] += 16 * o.n_dma
                    o.ticket = dcount[o.dma]
                    o.sem = dsem[o.dma]
                elif o.signal:
                    c += 1
                    o.ticket = c
                    o.sem = esem[e]
        final_deps = self.final_deps
        ops = self.ops

        def collect(deps):
            need = {}
            for d in deps:
                key = d.sem.num
                if need.get(key, (None, 0))[1] < d.ticket:
                    need[key] = (d.sem, d.ticket)
            return need

        def run(eng_name, eng):
            known = {}
            for o in ops[eng_name]:
                for key, (sem, val) in collect(o.deps).items():
                    if known.get(key, 0) >= val:
                        continue
                    eng.wait_ge(sem, val)
                    known[key] = val
                inst = o.fn(eng)
                if o.dma is None and o.signal:
                    inst.then_inc(o.sem, 1)
            if eng_name == "sp":
                for key, (sem, val) in collect(final_deps).items():
                    eng.wait_ge(sem, val)

        with nc.Block() as block:
            @block.tensor
            def _(e):
                run("pe", e)

            @block.scalar
            def _(e):
                run("act", e)

            @block.vector
            def _(e):
                run("dve", e)

            @block.gpsimd
            def _(e):
                run("pool", e)

            @block.sync
            def _(e):
                run("sp", e)


def build_program(n_groups=16, do_c=True, debug=False):
    nc = bass.Bass("TRN2", target_bir_lowering=False)

    def din(name, shape, dt=F32):
        return nc.dram_tensor(name, shape, dt, kind="ExternalInput").ap()

    xk = din("xk", [S_LEN, D])
    p_own = din("p_own", [2048, PLE])
    w_in = din("w_in", [D, 2048])
    w_pool = din("w_pool", [128, 512])
    w_out = din("w_out", [D, D])
    w_up = din("w_up", [D, DFF])
    w_down = din("w_down", [DFF, D])
    w_gate = din("w_gate", [D, D])
    w_ple = din("w_ple", [PLE, D])
    gcols = din("gcols", [128, 24])
    g_post = din("g_post", [3, D])
    consts = din("consts", [128, 512])
    rc16 = din("rc16", [128, 64])
    y_out = nc.dram_tensor("y_out", [2048, D], F32, kind="ExternalOutput").ap()
    yscr = nc.dram_tensor("yscr", [16, 128, 1024], BF16,
                          kind=("ExternalOutput" if debug else "Internal")).ap()

    S = Sched()
    with contextlib.ExitStack() as st:
        def sb(name, shape, dt):
            return st.enter_context(nc.sbuf_tensor(name, shape, dt))

        BIG = sb("BIG", [128, 65536], BF16)
        WIN = sb("WIN", [128, 16384], BF16)
        F4 = [sb("F4a", [128, 1024], F32), sb("F4b", [128, 1024], F32)]
        HNT = sb("HNT", [128, 4096], BF16)
        GB = sb("GB", [128, 1548], F32)
        SBF = sb("SBF", [128, 1032], F32)
        ABF = sb("ABF", [128, 1560], BF16)
        ATS = sb("ATS", [128, 1024], BF16)
        QTB = sb("QTB", [128, 1024], BF16)
        YP1 = sb("YP1", [128, 512], BF16)
        UT = sb("UT", [128, 576], F32)
        T12 = sb("T12", [128, 288], F32)
        DT = sb("DT", [128, 512], BF16)
        YH = sb("YH", [128, 2048], BF16)
        CST = sb("CST", [128, 512], BF16)
        WPL = sb("WPL", [128, 512], BF16)
        RC = sb("RC", [128, 64], F32)
        GC = sb("GC", [128, 24], F32)
        ST = sb("ST", [128, 32], F32)
        ONE = sb("ONE", [128, 1], F32)
        banks = [st.enter_context(nc.psum_tensor("B%d" % i, [128, 512], F32)) for i in range(8)]

        KT = BIG[:, 0:32768].rearrange("p (a t) -> p a t", a=4)
        V = BIG[:, 32768:65536].rearrange("p (a c) -> p a c", a=64)
        WUP = BIG[:, 0:32768].rearrange("p (a c) -> p a c", a=8)
        WDN = BIG[:, 32768:65536].rearrange("p (a c) -> p a c", a=32)
        WINV = WIN[:, :].rearrange("p (a c) -> p a c", a=8)
        WOUT = WIN[:, 0:8192].rearrange("p (a c) -> p a c", a=8)
        WGT = WIN[:, 8192:16384].rearrange("p (a c) -> p a c", a=8)
        hnT = HNT[:, :].rearrange("p (a t) -> p a t", a=8)
        yh = HNT[:, 0:2048].rearrange("p (j c t) -> p j c t", j=2, c=8)
        hnT2 = HNT[:, 2048:4096].rearrange("p (a t) -> p a t", a=8)
        gb = [GB[:, k * 516:k * 516 + 513] for k in range(3)]
        sbf = [SBF[:, 0:513], SBF[:, 516:1029]]
        sh = [ABF[:, k * 520:k * 520 + 513] for k in range(3)]
        sq0 = ABF[:, 0:512]
        ats = [ATS[:, 0:512], ATS[:, 512:1024]]
        QT2 = [QTB[:, k * 512:(k + 1) * 512].rearrange("p (a t) -> p a t", a=4) for k in range(2)]
        uT = UT[:, :].rearrange("p (a t) -> p a t", a=4)
        tt = [T12[:, 0:144], T12[:, 144:288]]
        dT = DT[:, :].rearrange("p (a t) -> p a t", a=4)
        ypool = [YH[:, 0:512], YP1[:, :]]
        ysb = YH[:, 512:1024]
        sqT = DT[:, :]
        yT3 = YH[:, 0:1024].rearrange("p (a t) -> p a t", a=8)
        hn = YH[:, 1024:2048]
        WPLE = YH[:, :].rearrange("p (a c) -> p a c", a=2)
        ident = CST[:, 0:128]
        negmask = CST[:, 128:256]
        blockones = CST[:, 256:384]
        negident = CST[:, 384:512]
        wpool = WPL[:, :].rearrange("p (a d) -> p a d", a=4)
        rc = RC[:, :].rearrange("p (a t) -> p a t", a=4)
        Bbf = [b[:, :].bitcast(BF16) for b in banks]
        tmpC = GB[:, 0:1024]
        gainC = SBF[:, 0:1024]
        hnC = ABF[:, 0:1024]
        actT = ATS[:, :].rearrange("p (a t) -> p a t", a=4)
        pbC = QTB[:, 0:256]
        pTC = QTB[:, 256:512].rearrange("p (a t) -> p a t", a=2)
        rC = [UT[:, 0:256], UT[:, 256:512]]
        pC = T12[:, 0:256]

        S.dma("pool", "cst", [(CST[:, :], consts)], writes=["cst"])
        S.dma("pool", "wpl", [(WPL[:, :], w_pool)], writes=["wpool"])
        S.dma("sp", "rc", [(RC[:, :], rc16)], writes=["rc"])
        S.dma("sp", "gc", [(GC[:, :], gcols)], writes=["gc"])
        S.op("pool", lambda e: e.memset(ONE[:, :], 1.0), writes=["one"])
        S.op("pool", lambda e: e.memset(GB[:, 0:1], 1.0), writes=["gb0"])
        S.op("pool", lambda e: e.memset(GB[:, 516:517], 1.0), writes=["gb1"])
        S.op("pool", lambda e: e.memset(GB[:, 1032:1033], 1.0), writes=["gb2"])

        for kc in range(8):
            for hf in range(2):
                i = kc * 2 + hf
                buf = F4[i % 2]
                S.dma("sp", "f4%d" % (i % 2),
                      [(buf[:, :], w_in[kc * 128:(kc + 1) * 128, hf * 1024:(hf + 1) * 1024])],
                      writes=["f4%d" % (i % 2)])
                eng = "act" if i % 2 == 0 else "dve"
                if eng == "act":
                    S.op("act", lambda e, kc=kc, hf=hf, buf=buf: e.activation(
                        out=WINV[:, kc, hf * 1024:(hf + 1) * 1024], in_=buf[:, :], func=AF.Copy,
                        scale=GC[:, kc:kc + 1]),
                        reads=["f4%d" % (i % 2), "gc"], writes=["win"])
                else:
                    S.op("dve", lambda e, kc=kc, hf=hf, buf=buf: e.tensor_scalar(
                        out=WINV[:, kc, hf * 1024:(hf + 1) * 1024], in0=buf[:, :],
                        scalar1=GC[:, kc:kc + 1], scalar2=None, op0=ALU.mult),
                        reads=["f4%d" % (i % 2), "gc"], writes=["win"])

        I32 = mybir.dt.int32
        STI = ST[:, :].bitcast(I32)

        def rstd_thunks(ss_col, c0, scale, tag, iters):
            y, a_, t_, u_ = c0, c0 + 1, c0 + 2, c0 + 3
            col = lambda c: ST[:, c:c + 1]
            coli = lambda c: STI[:, c:c + 1]
            th = []
            th.append(lambda: S.op("dve", lambda e: e.tensor_scalar(
                out=col(a_), in0=col(ss_col), scalar1=scale, scalar2=EPS, op0=ALU.mult, op1=ALU.add),
                reads=["ss" + tag], writes=["a" + tag]))
            th.append(lambda: S.op("dve", lambda e: e.tensor_scalar(
                out=coli(y), in0=coli(a_), scalar1=1, scalar2=None, op0=ALU.logical_shift_right),
                reads=["a" + tag], writes=["r" + tag]))
            th.append(lambda: S.op("dve", lambda e: e.tensor_scalar(
                out=coli(y), in0=coli(y), scalar1=-1, scalar2=0x5f3759df, op0=ALU.mult, op1=ALU.add),
                reads=["r" + tag], writes=["r" + tag]))
            for _ in range(iters):
                th.append(lambda: S.op("dve", lambda e: e.scalar_tensor_tensor(
                    out=col(t_), in0=col(y), scalar=col(a_), in1=col(y), op0=ALU.mult, op1=ALU.mult),
                    reads=["r" + tag, "a" + tag], writes=["t" + tag]))
                th.append(lambda: S.op("dve", lambda e: e.scalar_tensor_tensor(
                    out=col(u_), in0=col(t_), scalar=-0.5, in1=col(y), op0=ALU.mult, op1=ALU.mult),
                    reads=["t" + tag, "r" + tag], writes=["u" + tag]))
                th.append(lambda: S.op("dve", lambda e: e.scalar_tensor_tensor(
                    out=col(y), in0=col(y), scalar=1.5, in1=col(u_), op0=ALU.mult, op1=ALU.add),
                    reads=["u" + tag, "r" + tag], writes=["r" + tag]))
            return th

        def rstd_act(ss_col, t_col, r_col, scale, tag):
            S.op("dve", lambda e: e.tensor_scalar(out=ST[:, t_col:t_col + 1], in0=ST[:, ss_col:ss_col + 1],
                                                  scalar1=scale, scalar2=EPS, op0=ALU.mult, op1=ALU.add),
                 reads=["ss" + tag], writes=["t" + tag])
            S.op("act", lambda e: e.activation(out=ST[:, t_col:t_col + 1], in_=ST[:, t_col:t_col + 1],
                                               func=AF.Sqrt),
                 reads=["t" + tag], writes=["t" + tag])
            S.op("dve", lambda e: e.reciprocal(out=ST[:, r_col:r_col + 1], in_=ST[:, t_col:t_col + 1]),
                 reads=["t" + tag], writes=["r" + tag])

        xcnt = [0]
        step_ctr = [0]

        junkA = SBF[:, 516:1028].bitcast(BF16)

        def proj_kv(m):
            out = []
            tag = "A"
            for bi in range(4):
                pos = 4 * m + bi
                xi = xcnt[0] % 2
                xcnt[0] += 1
                xt = F4[xi]
                xn = "f4%d" % xi
                out.append(lambda xt=xt, xn=xn, pos=pos: S.dma(
                    "sp", xn, [(xt[:, :], xk[pos * 128:(pos + 1) * 128, :])], writes=[xn]))
                out.append(lambda xt=xt, xn=xn, bi=bi: S.op(
                    "act", lambda e: e.activation(out=junkA, in_=xt[:, :], func=AF.Square,
                                                  accum_out=ST[:, bi:bi + 1]),
                    reads=[xn], writes=["junkA", "ssA%d" % bi]))
            out.append(lambda: S.op(
                "dve", lambda e: e.tensor_scalar(out=ST[:, 28:32], in0=ST[:, 0:4], scalar1=1.0 / D, scalar2=EPS,
                                                 op0=ALU.mult, op1=ALU.add),
                reads=["ssA0", "ssA1", "ssA2", "ssA3"], writes=["tA"]))
            out.append(lambda: S.op(
                "act", lambda e: e.activation(out=ST[:, 28:32], in_=ST[:, 28:32], func=AF.Sqrt),
                reads=["tA"], writes=["tA"]))
            out.append(lambda: S.op(
                "dve", lambda e: e.reciprocal(out=ST[:, 24:28], in_=ST[:, 28:32]),
                reads=["tA"], writes=["rA"]))
            per_block = []
            for bi in range(4):
                th = []
                pos = 4 * m + bi
                xi = xcnt[0] % 2
                xcnt[0] += 1
                xt = F4[xi]
                xn = "f4%d" % xi
                th.append(lambda xt=xt, xn=xn, pos=pos: S.dma(
                    "sp", xn, [(xt[:, :], xk[pos * 128:(pos + 1) * 128, :])], writes=[xn]))
                th.append(lambda xt=xt, xn=xn, bi=bi: S.op(
                    "act", lambda e: e.activation(out=hn, in_=xt[:, :], func=AF.Copy, scale=ST[:, 24 + bi:25 + bi]),
                    reads=[xn, "rA"], writes=["hn"]))

                for h2 in range(2):
                    def tr4(e, h2=h2):
                        inst = None
                        for kc in range(4 * h2, 4 * h2 + 4):
                            inst = e.transpose(Bbf[0][:, kc * 128:(kc + 1) * 128],
                                               hn[:, kc * 128:(kc + 1) * 128], ident)
                        return inst
                    th.append(lambda tr4=tr4: S.op("pe", tr4, reads=["hn", "cst"], writes=["b0"]))
                th.append(lambda bi=bi: S.op(
                    "dve", lambda e: e.tensor_copy(out=hnT[:, :, bi * 128:(bi + 1) * 128],
                                                   in_=Bbf[0][:, :].rearrange("p (a t) -> p a t", a=8)),
                    reads=[], writes=["b0", "hnT%d" % bi]))
                for q4 in range(4):
                    def vmm(e, bi=bi, q4=q4):
                        inst = None
                        for kc in range(2 * q4, 2 * q4 + 2):
                            inst = e.matmul(banks[1][:, :], lhsT=hnT[:, kc, bi * 128:(bi + 1) * 128],
                                            rhs=WINV[:, kc, 1536:2048], start=(kc == 0), stop=(kc == 7))
                        return inst
                    th.append(lambda vmm=vmm, bi=bi: S.op("pe", vmm, reads=["hnT%d" % bi, "win"], writes=["b1"]))
                th.append(lambda pos=pos, m=m: S.op(
                    "act", lambda e: e.activation(out=V[:, pos, :], in_=banks[1][:, :], func=AF.Copy),
                    reads=[], writes=["b1", "V%d" % m]))
                per_block.append(th)
            stagger = 6
            keyed = []
            for bi, th in enumerate(per_block):
                for k, t_ in enumerate(th):
                    keyed.append((k + bi * stagger, bi, k, t_))
            keyed.sort(key=lambda z: (z[0], z[1]))
            out += [z[3] for z in keyed]
            allT = ["hnT%d" % bi for bi in range(4)]
            for pr in range(4):
                for q4 in range(4):
                    def kmm(e, pr=pr, q4=q4):
                        inst = None
                        for kc in range(2 * q4, 2 * q4 + 2):
                            inst = e.matmul(banks[1][:, :],
                                            lhsT=WINV[:, kc, 1024 + pr * 128:1024 + (pr + 1) * 128],
                                            rhs=hnT[:, kc, :], start=(kc == 0), stop=(kc == 7))
                        return inst
                    out.append(lambda kmm=kmm: S.op("pe", kmm, reads=allT + ["win"], writes=["b1"]))
                out.append(lambda pr=pr, m=m: S.op(
                    "dve", lambda e: e.tensor_copy(out=KT[:, pr, m * 512:(m + 1) * 512], in_=banks[1][:, :]),
                    reads=[], writes=["b1", "KT%d" % m]))
            return out

        def proj_own(m):
            qp = m % 2
            th = []

            def t_qp(pr):
                def qmm(e):
                    inst = None
                    for kc in range(8):
                        inst = e.matmul(banks[1][:, pr * 128:(pr + 1) * 128],
                                        lhsT=WINV[:, kc, 512 + pr * 128:512 + (pr + 1) * 128],
                                        rhs=hnT[:, kc, 0:128], start=(kc == 0), stop=(kc == 7))
                    return inst
                S.op("pe", qmm, reads=["hnT0", "win"], writes=["b1"])
            for pr in range(4):
                th.append(lambda pr=pr: t_qp(pr))
            th.append(lambda: S.op(
                "dve", lambda e: e.tensor_copy(out=QTB[:, qp * 512:(qp + 1) * 512], in_=banks[1][:, :]),
                reads=[], writes=["b1", "QT%d" % qp]))

            def t_u(rd):
                def umm(e):
                    inst = None
                    for q in range(2):
                        gi = rd * 2 + q
                        for kc in range(8):
                            inst = e.matmul(banks[1][:, q * 144:(q + 1) * 144],
                                            lhsT=WINV[:, kc, gi * 128:(gi + 1) * 128],
                                            rhs=hnT[:, kc, 0:144], start=(kc == 0), stop=(kc == 7))
                    return inst
                S.op("pe", umm, reads=["hnT0", "hnT1", "win"], writes=["b1"])
                S.op("act", lambda e: e.activation(out=UT[:, rd * 288:(rd + 1) * 288],
                                                   in_=banks[1][:, 0:288], func=AF.Copy),
                     reads=[], writes=["b1", "uT"])
            th.append(lambda: t_u(0))
            th.append(lambda: t_u(1))

            def t_mix(gi):
                a_ = uT[:, gi, :]
                w = 2 ** (gi + 1)
                S.op("dve", lambda e: e.tensor_tensor(out=tt[0][:, 0:143], in0=a_[:, 0:143],
                                                      in1=a_[:, 1:144], op=ALU.add),
                     reads=["uT"], writes=["tt0"])
                cur, ln, sh_ = 0, 143, 2
                for lvl in range(gi):
                    nl = ln - sh_
                    S.op("dve", lambda e, cur=cur, nl=nl, sh_=sh_: e.tensor_tensor(
                        out=tt[1 - cur][:, 0:nl], in0=tt[cur][:, 0:nl], in1=tt[cur][:, sh_:sh_ + nl], op=ALU.add),
                        reads=["tt%d" % cur], writes=["tt%d" % (1 - cur)])
                    cur, ln, sh_ = 1 - cur, nl, sh_ * 2
                S.op("dve", lambda e, cur=cur: e.scalar_tensor_tensor(
                    out=dT[:, gi, :], in0=tt[cur][:, 0:128], scalar=1.0 / w, in1=a_[:, 0:128],
                    op0=ALU.mult, op1=ALU.subtract),
                    reads=["tt%d" % cur, "uT"], writes=["dT"])
                if m == 15:
                    S.op("dve", lambda e, cur=cur: e.tensor_tensor(
                        out=tt[1 - cur][:, 0:16], in0=tt[cur][:, 112:128], in1=rc[:, gi, :], op=ALU.mult),
                        reads=["tt%d" % cur, "rc"], writes=["tt%d" % (1 - cur)])
                    S.op("dve", lambda e, cur=cur: e.tensor_tensor(
                        out=dT[:, gi, 112:128], in0=tt[1 - cur][:, 0:16], in1=a_[:, 112:128], op=ALU.subtract),
                        reads=["tt%d" % (1 - cur), "uT"], writes=["dT"])
            for gi in range(4):
                th.append(lambda gi=gi: t_mix(gi))

            def t_p():
                def pmm(e):
                    inst = None
                    for gi in range(4):
                        inst = e.matmul(banks[1][:, gi * 128:(gi + 1) * 128], lhsT=wpool[:, gi, :],
                                        rhs=dT[:, gi, :], start=True, stop=True)
                    return inst
                S.op("pe", pmm, reads=["dT", "wpool"], writes=["b1"])
                S.op("act", lambda e: e.activation(out=ypool[qp], in_=banks[1][:, :], func=AF.Copy),
                     reads=[], writes=["b1", "yp%d" % qp])
            th.append(t_p)
            return th

        def attention(m, bg):
            sp_ = (15 - m) % 2
            oT = banks[6 + sp_]
            oTn = "oT%d" % sp_
            nch = 16 - m
            steps = []
            for pr in range(4):
                for c in range(nch):
                    for hh in range(2):
                        steps.append((pr, hh, c))
            K = len(steps)
            base = step_ctr[0]
            step_ctr[0] += K
            QT = QT2[m % 2]
            QTn = "QT%d" % (m % 2)

            def emit_qk(i):
                pr, hh, c = steps[i]
                par = (base + i) % 2
                g3 = (base + i) % 3
                lo, hi = hh * 64, hh * 64 + 64
                k0 = (4 * m + 4 * c) * 128

                def f(e):
                    inst = e.matmul(banks[2 + par][:, :], lhsT=QT[lo:hi, pr, :], rhs=KT[lo:hi, pr, k0:k0 + 512],
                                    start=True, stop=(c != 0))
                    if c == 0:
                        inst = e.matmul(banks[2 + par][:, 0:128], lhsT=ident, rhs=negmask,
                                        start=False, stop=True)
                    return inst
                S.op("pe", f, reads=[QTn, "KT%d" % (m + c), "cst"], writes=["z%d" % par])
                S.op("act", lambda e: e.activation(out=gb[g3][:, 1:513], in_=banks[2 + par][:, :],
                                                   func=AF.Sigmoid, scale=-0.125),
                     reads=[], writes=["z%d" % par, "gb%d" % g3])

            def emit_scan(i):
                pr, hh, c = steps[i]
                g3 = (base + i) % 3
                if c == 0:
                    init = 1.0
                    rds = ["gb%d" % g3, "one"]
                else:
                    p3 = (base + i - 2) % 3
                    init = sh[p3][:, 512:513]
                    rds = ["gb%d" % g3, "one", "sh%d" % p3]
                S.op("dve", lambda e: e.tensor_tensor_scan(
                    out=sh[g3], data0=gb[g3], data1=ONE[:, 0:1].to_broadcast([128, 513]),
                    initial=init, op0=ALU.mult, op1=ALU.mult),
                    reads=rds, writes=["sh%d" % g3])

            def emit_tr(i):
                par = (base + i) % 2
                s3 = (base + i) % 3

                def f(e):
                    inst = None
                    for j in range(4):
                        e.matmul(banks[4 + par][:, j * 128:(j + 1) * 128], lhsT=sh[s3][:, j * 128:j * 128 + 128],
                                 rhs=ident, start=True, stop=False)
                        inst = e.matmul(banks[4 + par][:, j * 128:(j + 1) * 128],
                                        lhsT=sh[s3][:, j * 128 + 1:j * 128 + 129],
                                        rhs=negident, start=False, stop=True)
                    return inst
                S.op("pe", f, reads=["sh%d" % s3, "cst"], writes=["atp%d" % par])
                S.op("act", lambda e: e.activation(out=ats[par], in_=banks[4 + par][:, :], func=AF.Copy),
                     reads=[], writes=["atp%d" % par, "ats%d" % par])

            def emit_av(i):
                pr, hh, c = steps[i]
                par = (base + i) % 2
                head = 2 * pr + hh
                lo, hi = hh * 64, hh * 64 + 64

                def f(e):
                    inst = None
                    for j in range(4):
                        pos = 4 * m + 4 * c + j
                        inst = e.matmul(oT[lo:hi, pr * 128:(pr + 1) * 128],
                                        lhsT=V[:, pos, head * 64:(head + 1) * 64],
                                        rhs=ats[par][:, j * 128:(j + 1) * 128],
                                        start=(c == 0 and j == 0), stop=(c == nch - 1 and j == 3))
                    return inst
                S.op("pe", f, reads=["ats%d" % par, "V%d" % (m + c)], writes=[oTn])

            n_it = K + 5
            done = 0
            for it, i in enumerate(range(-3, K + 2)):
                if 0 <= i < K:
                    emit_scan(i)
                if 0 <= i + 3 < K:
                    emit_qk(i + 3)
                if 0 <= i - 1 < K:
                    emit_tr(i - 1)
                if 0 <= i - 2 < K:
                    emit_av(i - 2)
                tgt = (len(bg) * (it + 1)) // n_it
                while done < tgt:
                    bg[done]()
                    done += 1
            while done < len(bg):
                bg[done]()
                done += 1

        def tail_thunks(m):
            sp_ = (15 - m) % 2
            oT = banks[6 + sp_]
            oTn = "oT%d" % sp_
            qp = m % 2
            th = []
            th.append(lambda: S.op("act", lambda e: e.activation(out=sqT, in_=oT[:, :], func=AF.Square),
                                   reads=[], writes=[oTn, "dT"]))
            th.append(lambda: S.op("pe", lambda e: e.matmul(banks[1][:, :], lhsT=blockones, rhs=sqT,
                                                            start=True, stop=True),
                                   reads=["dT", "cst"], writes=["b1"]))
            th.append(lambda: S.op("dve", lambda e: e.tensor_scalar(out=sbf[0][:, 0:512], in0=banks[1][:, :],
                                                                    scalar1=EPS, scalar2=None, op0=ALU.add),
                                   reads=[], writes=["b1", "sb0"]))
            th.append(lambda: S.op("act", lambda e: e.activation(out=sbf[0][:, 0:512], in_=sbf[0][:, 0:512],
                                                                 func=AF.Sqrt),
                                   reads=["sb0"], writes=["sb0"]))
            th.append(lambda: S.op("dve", lambda e: e.reciprocal(out=sbf[0][:, 0:512], in_=sbf[0][:, 0:512]),
                                   reads=["sb0"], writes=["sb0"]))
            th.append(lambda: S.op("dve", lambda e: e.tensor_tensor(out=ysb, in0=oT[:, :],
                                                                    in1=sbf[0][:, 0:512], op=ALU.mult),
                                   reads=["sb0"], writes=[oTn, "ysb"]))
            th.append(lambda: S.dma("sp", "yscr", [(yscr[m][:, 0:512], ypool[qp]), (yscr[m][:, 512:1024], ysb)],
                                    reads=["yp%d" % qp, "ysb"], writes=["yscr%d" % m]))
            return th

        stg = [0]

        def stage(dst_view, src_ap, gcol, wname, extra=()):
            i = stg[0]
            stg[0] += 1
            buf = F4[i % 2]
            bn = "f4%d" % (i % 2)
            S.dma("sp", bn, [(buf[:, :], src_ap)], writes=[bn])
            if i % 2 == 0:
                S.op("act", lambda e: e.activation(out=dst_view, in_=buf[:, :], func=AF.Copy,
                                                   scale=GC[:, gcol:gcol + 1]),
                     reads=[bn, "gc"], writes=[wname] + list(extra))
            else:
                S.op("dve", lambda e: e.tensor_scalar(out=dst_view, in0=buf[:, :], scalar1=GC[:, gcol:gcol + 1],
                                                      scalar2=None, op0=ALU.mult),
                     reads=[bn, "gc"], writes=[wname] + list(extra))

        def win_loads():
            th = []
            for kc in range(8):
                th.append(lambda kc=kc: stage(WOUT[:, kc, :], w_out[kc * 128:(kc + 1) * 128, :], 8 + kc, "win"))
            th.append(lambda: S.dma(
                "pool", "wgt", [(WGT[:, kc * 4:(kc + 1) * 4, :],
                                 w_gate[kc * 512:(kc + 1) * 512, :].rearrange("(c p) n -> p c n", p=128))
                                for kc in range(2)], writes=["win"]))
            return th

        m_lo = 16 - n_groups
        for t_ in proj_kv(15) + proj_own(15):
            t_()
        win_done = False
        prev_tail = []
        for m_ in range(15, m_lo - 1, -1):
            bg = list(prev_tail)
            if m_ - 1 >= m_lo:
                bg += proj_kv(m_ - 1) + proj_own(m_ - 1)
            elif do_c:
                bg += win_loads()
                win_done = True
            attention(m_, bg)
            prev_tail = tail_thunks(m_)
        for t_ in prev_tail:
            t_()

        if not do_c:
            o = S.dma("sp", "yout", [(y_out[0:128, 0:512], UT[:, 0:512])], reads=["uT"])
            S.finish([o])
            S.emit(nc, st)
            return nc

        allKV = ["KT%d" % i for i in range(16)] + ["V%d" % i for i in range(16)]
        if not win_done:
            for t_ in win_loads():
                t_()
        S.dma("pool", "wple", [(WPLE[:, :, :], w_ple.rearrange("(c p) n -> p c n", p=128))],
              reads=[], writes=["yp0", "ysb", "hn"])
        npiece = 0
        for q in range(4):
            for kc in range(8):
                stage(WUP[:, kc, q * 1024:(q + 1) * 1024],
                      w_up[kc * 128:(kc + 1) * 128, q * 1024:(q + 1) * 1024], 16 + kc, "wup%d" % q,
                      extra=(allKV if npiece < 2 else ()))
                npiece += 1
        for c4 in range(8):
            S.dma("pool", "wdn%d" % c4,
                  [(WDN[:, c4 * 4:(c4 + 1) * 4, :],
                    w_down[c4 * 512:(c4 + 1) * 512, :].rearrange("(c p) n -> p c n", p=128))],
                  writes=(allKV if c4 == 0 else []) + ["wdn%d" % c4])

        outs = []
        gidx = {"mix": 0, "mlp": 1, "ple": 2}

        cur_gain = [None]

        def load_gain(which):
            if cur_gain[0] == which:
                return
            cur_gain[0] = which
            S.dma("sp", "gain", [(gainC, g_post[gidx[which]:gidx[which] + 1, :].partition_broadcast(128))],
                  writes=["gain"])

        def norm_residual(src_banks, xt, xn, which, tagc):
            for hf in range(2):
                S.op("act", lambda e, hf=hf: e.activation(out=hnC[:, hf * 512:(hf + 1) * 512],
                                                          in_=banks[src_banks[hf]][:, :], func=AF.Square,
                                                          accum_out=ST[:, 4 + hf:5 + hf]),
                     reads=[], writes=["b%d" % src_banks[hf], "hnC", "ssq%d" % hf])
            S.op("dve", lambda e: e.tensor_tensor(out=ST[:, 6:7], in0=ST[:, 4:5], in1=ST[:, 5:6], op=ALU.add),
                 reads=["ssq0", "ssq1"], writes=["ssN"])
            rstd_act(6, 7, 8, 1.0 / D, "N")
            load_gain(which)
            for hf in range(2):
                S.op("dve", lambda e, hf=hf: e.scalar_tensor_tensor(
                    out=tmpC[:, hf * 512:(hf + 1) * 512], in0=banks[src_banks[hf]][:, :], scalar=ST[:, 8:9],
                    in1=gainC[:, hf * 512:(hf + 1) * 512], op0=ALU.mult, op1=ALU.mult),
                    reads=["rN", "gain"], writes=["b%d" % src_banks[hf], "tmpC"])
            if xt is not None:
                S.op("dve", lambda e: e.tensor_tensor(out=xt[:, :], in0=xt[:, :], in1=tmpC, op=ALU.add),
                     reads=["tmpC", xn], writes=[xn])

        def to_T(src_ap, dst3, nk, src_name, dst_name):
            def f(e):
                inst = None
                for kc in range(nk):
                    inst = e.transpose(Bbf[0][:, kc * 128:(kc + 1) * 128], src_ap[:, kc * 128:(kc + 1) * 128], ident)
                return inst
            S.op("pe", f, reads=[src_name, "cst"], writes=["b0"])
            S.op("dve", lambda e: e.tensor_copy(
                out=dst3, in_=Bbf[0][:, 0:nk * 128].rearrange("p (a t) -> p a t", a=nk)),
                reads=[], writes=["b0", dst_name])

        for t in range(8):
            accb = [[4, 5], [6, 7]]
            for j in range(2):
                mblk = 2 * t + j
                if t == 0:
                    S.dma("sp", "yh%d" % j, [(HNT[:, j * 1024:(j + 1) * 1024], yscr[mblk])],
                          reads=["yscr%d" % mblk], writes=["yh%d" % j])
                S.dma("sp", "f4%d" % j, [(F4[j][:, :], xk[(4 * mblk) * 128:(4 * mblk + 1) * 128, :])],
                      writes=["f4%d" % j])
            for j in range(2):
                def f(e, j=j):
                    inst = None
                    for hf in range(2):
                        for kc in range(8):
                            inst = e.matmul(banks[accb[j][hf]][:, :], lhsT=yh[:, j, kc, :],
                                            rhs=WOUT[:, kc, hf * 512:(hf + 1) * 512],
                                            start=(kc == 0), stop=(kc == 7))
                    return inst
                S.op("pe", f, reads=["yh%d" % j, "win"], writes=["b%d" % accb[j][0], "b%d" % accb[j][1]])
                norm_residual(accb[j], F4[j], "f4%d" % j, "mix", "m")
            if t < 7:
                for j in range(2):
                    nblk = 2 * (t + 1) + j
                    S.dma("sp", "yh%d" % j, [(HNT[:, j * 1024:(j + 1) * 1024], yscr[nblk])],
                          reads=["yscr%d" % nblk], writes=["yh%d" % j])
            for j in range(2):
                S.op("act", lambda e, j=j: e.activation(out=hnC, in_=F4[j][:, :], func=AF.Square,
                                                        accum_out=ST[:, 14:15]),
                     reads=["f4%d" % j], writes=["hnC", "ssM"])
                rstd_act(14, 15, 16, 1.0 / D, "M")
                S.op("act", lambda e, j=j: e.activation(out=hnC, in_=F4[j][:, :], func=AF.Copy,
                                                        scale=ST[:, 16:17]),
                     reads=["f4%d" % j, "rM"], writes=["hnC"])
                to_T(hnC, hnT2[:, :, j * 128:(j + 1) * 128], 8, "hnC", "hnT2")
            def up(fc):
                ub = 2 + fc % 2

                def f(e):
                    inst = None
                    for kc in range(8):
                        inst = e.matmul(banks[ub][:, 0:256], lhsT=WUP[:, kc, fc * 128:(fc + 1) * 128],
                                        rhs=hnT2[:, kc, :], start=(kc == 0), stop=(kc == 7))
                    return inst
                S.op("pe", f, reads=["hnT2", "wup%d" % (fc // 8)], writes=["b%d" % ub])
                rr = rC[fc % 2]
                S.op("act", lambda e: e.activation(out=rr, in_=banks[ub][:, 0:256], func=AF.Relu),
                     reads=[], writes=["b%d" % ub, "r%d" % (fc % 2)])
                S.op("dve", lambda e: e.tensor_tensor(out=actT[:, fc % 4, :], in0=rr, in1=rr, op=ALU.mult),
                     reads=["r%d" % (fc % 2)], writes=["act%d" % (fc % 4)])

            def down(fc):
                def f(e):
                    inst = None
                    for j in range(2):
                        for hf in range(2):
                            inst = e.matmul(banks[accb[j][hf]][:, :], lhsT=actT[:, fc % 4, j * 128:(j + 1) * 128],
                                            rhs=WDN[:, fc, hf * 512:(hf + 1) * 512],
                                            start=(fc == 0), stop=(fc == 31))
                    return inst
                S.op("pe", f, reads=["act%d" % (fc % 4), "wdn%d" % (fc // 4)], writes=["b4", "b5", "b6", "b7"])

            for i in range(34):
                if i < 32:
                    up(i)
                if 0 <= i - 2 < 32:
                    down(i - 2)
            for j in range(2):
                norm_residual(accb[j], F4[j], "f4%d" % j, "mlp", "d")
            for j in range(2):
                mblk = 2 * t + j
                S.op("act", lambda e, j=j: e.activation(out=hnC, in_=F4[j][:, :], func=AF.Copy),
                     reads=["f4%d" % j], writes=["hnC"])
                to_T(hnC, hnT2[:, :, j * 128:(j + 1) * 128], 8, "hnC", "hnT2")
                S.dma("sp", "pC", [(pC, p_own[mblk * 128:(mblk + 1) * 128, :])], writes=["pC"])
                S.op("dve", lambda e: e.tensor_copy(out=pbC, in_=pC), reads=["pC"], writes=["pbC"])
                to_T(pbC, pTC, 2, "pbC", "pTC")

                def f(e, j=j):
                    inst = None
                    for hf in range(2):
                        for kc in range(2):
                            inst = e.matmul(banks[2 + hf][:, :], lhsT=pTC[:, kc, :],
                                            rhs=WPLE[:, kc, hf * 512:(hf + 1) * 512],
                                            start=(kc == 0), stop=(kc == 1))
                    return inst
                S.op("pe", f, reads=["pTC", "ysb", "hn"], writes=["b2", "b3"])
                norm_residual([2, 3], None, None, "ple", "p")

                def fg(e, j=j):
                    inst = None
                    for hf in range(2):
                        for kc in range(8):
                            inst = e.matmul(banks[accb[j][hf]][:, :], lhsT=hnT2[:, kc, j * 128:(j + 1) * 128],
                                            rhs=WGT[:, kc, hf * 512:(hf + 1) * 512],
                                            start=(kc == 0), stop=(kc == 7))
                    return inst
                S.op("pe", fg, reads=["hnT2", "win"], writes=["b%d" % accb[j][0], "b%d" % accb[j][1]])
                cur_gain[0] = None
                for hf in range(2):
                    S.op("act", lambda e, j=j, hf=hf: e.activation(
                        out=gainC[:, hf * 512:(hf + 1) * 512], in_=banks[accb[j][hf]][:, :], func=AF.Sigmoid),
                        reads=[], writes=["b%d" % accb[j][hf], "gain"])
                S.op("dve", lambda e: e.tensor_tensor(out=tmpC, in0=tmpC, in1=gainC, op=ALU.mult),
                     reads=["tmpC", "gain"], writes=["tmpC"])
                S.op("dve", lambda e, j=j: e.tensor_tensor(out=F4[j][:, :], in0=F4[j][:, :], in1=tmpC, op=ALU.add),
                     reads=["tmpC", "f4%d" % j], writes=["f4%d" % j])
                outs.append(S.dma("sp", "yout%d" % j, [(y_out[mblk * 128:(mblk + 1) * 128, :], F4[j][:, :])],
                                  reads=["f4%d" % j]))
        S.finish(outs)
        S.emit(nc, st)
    return nc


def _consts():
    ident = np.eye(128, dtype=np.float32)
    p = np.arange(128)[:, None]
    i = np.arange(128)[None, :]
    negmask = np.where(i <= p, NEG, 0.0).astype(np.float32)
    blockones = np.where((p // 64) == (i // 64), 1.0 / 64.0, 0.0).astype(np.float32)
    return np.concatenate([ident, negmask, blockones, -ident], axis=1)


def _rc16(g):
    rc = np.zeros((4, 16), np.float32)
    for gi in range(4):
        w = 2 ** (gi + 1)
        for q in range(16):
            i = 112 + q
            if g == 0:
                cnt = min(128 - i, w)
            else:
                cnt = w
            rc[gi, q] = 1.0 / cnt
    return np.ascontiguousarray(np.broadcast_to(rc.reshape(1, 64), (128, 64)))


_PROG = {}


def _prepare(x, p, g_mix_pre, w_in, w_pool, pool_scale, g_sb, w_out, g_mix_post,
             g_mlp_pre, w_up, w_down, g_mlp_post, w_ple_gate, w_ple_proj, g_ple):
    f = np.float32
    x = np.asarray(x, f)
    p = np.asarray(p, f)

    def cols(v):
        return np.asarray(v, f).reshape(8, 128).T

    gcat = np.concatenate([np.asarray(pool_scale, f)[0], np.asarray(g_sb, f)[0]])
    gcols = np.ascontiguousarray(np.concatenate(
        [cols(np.asarray(g_mix_pre, f)[0]), cols(gcat), cols(np.asarray(g_mlp_pre, f)[0])], axis=1))
    g_post = np.ascontiguousarray(np.stack([np.asarray(g_mix_post, f)[0], np.asarray(g_mlp_post, f)[0],
                                            np.asarray(g_ple, f)[0]]))
    wpool = np.ascontiguousarray(np.asarray(w_pool, f)[0].transpose(1, 0, 2).reshape(128, 512))
    shared = {
        "w_in": np.ascontiguousarray(np.asarray(w_in, f)[0]),
        "w_pool": wpool,
        "w_out": np.ascontiguousarray(np.asarray(w_out, f)[0]),
        "w_up": np.ascontiguousarray(np.asarray(w_up, f)[0]),
        "w_down": np.ascontiguousarray(np.asarray(w_down, f)[0]),
        "w_gate": np.ascontiguousarray(np.asarray(w_ple_gate, f)[0]),
        "w_ple": np.ascontiguousarray(np.asarray(w_ple_proj, f)[0]),
        "gcols": gcols,
        "g_post": g_post,
        "consts": _consts(),
    }
    in_maps = []
    tok_maps = []
    for c in range(8):
        b, g = c // 4, c % 4
        t0 = (61 + g) * 128 - 1
        xk = np.zeros((S_LEN, D), f)
        n = t0 + 1
        xk[:n] = x[b, t0::-1][:n]
        toks = np.empty(2048, np.int64)
        for m in range(16):
            qb = 60 + g - 4 * m
            toks[m * 128:(m + 1) * 128] = qb * 128 + 127 - np.arange(128)
        tok_maps.append((b, toks))
        d = dict(shared)
        d["xk"] = xk
        d["p_own"] = np.ascontiguousarray(p[0, b][toks])
        d["rc16"] = _rc16(g)
        in_maps.append(d)
    return in_maps, tok_maps


def kernel(**inputs):
    if "nc" not in _PROG:
        _PROG["nc"] = build_program()
    nc = _PROG["nc"]
    in_maps, tok_maps = _prepare(**inputs)
    res = run_bass_kernel_spmd(nc, in_maps, core_ids=list(range(8)))
    f = np.float32
    out = np.empty((2, S_LEN, D), f)
    for c in range(8):
        b, toks = tok_maps[c]
        out[b, toks] = res.results[c]["y_out"]
    return out
```

```python
import contextlib
import numpy as np
import concourse.bass as bass
import concourse.mybir as mybir
from concourse.bass_utils import run_bass_kernel_spmd

F32 = mybir.dt.float32
BF16 = mybir.dt.bfloat16
AF = mybir.ActivationFunctionType
ALU = mybir.AluOpType

D = 1024
S_LEN = 8192
NB = 64
DFF = 4096
PLE = 256
EPS = 1e-6
NEG = -30000.0
ENGS = ("pe", "act", "dve", "pool", "sp")
SYNC_SAME_ENGINE = True
SUB_ENG = "pool"


class _Op:
    __slots__ = ("eng", "fn", "deps", "signal", "ticket", "sem", "dma", "n_dma")

    def __init__(self, eng, fn, dma, n_dma):
        self.eng = eng
        self.fn = fn
        self.deps = []
        self.signal = False
        self.ticket = None
        self.sem = None
        self.dma = dma
        self.n_dma = n_dma


class Sched:
    def __init__(self):
        self.ops = {e: [] for e in ENGS}
        self.last_w = {}
        self.readers = {}
        self.dma_keys = []
        self.final_deps = []
        self._dsem = {}

    def op(self, eng, fn, reads=(), writes=(), dma=None, n_dma=1):
        o = _Op(eng, fn, dma, n_dma)
        if dma is not None and dma not in self.dma_keys:
            self.dma_keys.append(dma)
        deps = []
        for r in reads:
            w = self.last_w.get(r)
            if w is not None:
                deps.append((w, "raw"))
        for r in writes:
            w = self.last_w.get(r)
            if w is not None:
                deps.append((w, "waw"))
            for rd in self.readers.get(r, ()):
                deps.append((rd, "war"))
        for d, kind in deps:
            if d is o:
                continue
            if d.eng == eng and d.dma is None and dma is None and kind != "raw" and not SYNC_SAME_ENGINE:
                continue
            if d.eng == "pe" and eng == "pe" and d.dma is None and dma is None:
                continue
            o.deps.append(d)
            d.signal = True
        for r in reads:
            self.readers.setdefault(r, []).append(o)
        for r in writes:
            self.last_w[r] = o
            self.readers[r] = []
        self.ops[eng].append(o)
        return o

    def dma(self, eng, key, pairs, reads=(), writes=()):
        sched = self

        def fn(e, pairs=pairs, key=key):
            inst = None
            for (o_ap, i_ap) in pairs:
                inst = e.dma_start(out=o_ap, in_=i_ap)
                inst.then_inc(sched._dsem[key], 16)
            return inst

        return self.op(eng, fn, reads=reads, writes=writes, dma=key, n_dma=len(pairs))

    def finish(self, ops):
        for o in ops:
            o.signal = True
            self.final_deps.append(o)

    def emit(self, nc, stack):
        esem = {e: stack.enter_context(nc.semaphore("s_" + e)) for e in ENGS}
        dsem = {k: stack.enter_context(nc.semaphore("d_%d" % i))
                for i, k in enumerate(self.dma_keys)}
        self._dsem = dsem
        dcount = {k: 0 for k in self.dma_keys}
        for e in ENGS:
            c = 0
            for o in self.ops[e]:
                if o.dma is not None:
                    dcount[o.dma] += 16 * o.n_dma
                    o.ticket = dcount[o.dma]
                    o.sem = dsem[o.dma]
                elif o.signal:
                    c += 1
                    o.ticket = c
                    o.sem = esem[e]
        final_deps = self.final_deps
        ops = self.ops

        def collect(deps):
            need = {}
            for d in deps:
                key = d.sem.num
                if need.get(key, (None, 0))[1] < d.ticket:
                    need[key] = (d.sem, d.ticket)
            return need

        def run(eng_name, eng):
            known = {}
            for o in ops[eng_name]:
                for key, (sem, val) in collect(o.deps).items():
                    if known.get(key, 0) >= val:
                        continue
                    eng.wait_ge(sem, val)
                    known[key] = val
                inst = o.fn(eng)
                if o.dma is None and o.signal:
                    inst.then_inc(o.sem, 1)
            if eng_name == "sp":
                for key, (sem, val) in collect(final_deps).items():
                    eng.wait_ge(sem, val)

        with nc.Block() as block:
            @block.tensor
            def _(e):
                run("pe", e)

            @block.scalar
            def _(e):
                run("act", e)

            @block.vector
            def _(e):
                run("dve", e)

            @block.gpsimd
            def _(e):
                run("pool", e)

            @block.sync
            def _(e):
                run("sp", e)


def build_program(n_groups=16, do_c=True, debug=False):
    nc = bass.Bass("TRN2", target_bir_lowering=False)

    def din(name, shape, dt=F32):
        return nc.dram_tensor(name, shape, dt, kind="ExternalInput").ap()

    xk = din("xk", [S_LEN, D])
    p_own = din("p_own", [2048, PLE])
    w_in = din("w_in", [D, 2048])
    w_pool = din("w_pool", [128, 512])
    w_out = din("w_out", [D, D])
    w_up = din("w_up", [D, DFF])
    w_down = din("w_down", [DFF, D])
    w_gate = din("w_gate", [D, D])
    w_ple = din("w_ple", [PLE, D])
    gcols = din("gcols", [128, 24])
    g_post = din("g_post", [3, D])
    consts = din("consts", [128, 512])
    rc16 = din("rc16", [128, 64])
    y_out = nc.dram_tensor("y_out", [2048, D], F32, kind="ExternalOutput").ap()
    yscr = nc.dram_tensor("yscr", [16, 128, 1024], BF16,
                          kind=("ExternalOutput" if debug else "Internal")).ap()

    S = Sched()
    with contextlib.ExitStack() as st:
        def sb(name, shape, dt):
            return st.enter_context(nc.sbuf_tensor(name, shape, dt))

        BIG = sb("BIG", [128, 65536], BF16)
        WIN = sb("WIN", [128, 16384], BF16)
        F4 = [sb("F4a", [128, 1024], F32), sb("F4b", [128, 1024], F32)]
        HNT = sb("HNT", [128, 4096], BF16)
        GB = sb("GB", [128, 1548], F32)
        SBF = sb("SBF", [128, 1032], F32)
        ABF = sb("ABF", [128, 1560], BF16)
        ATS = sb("ATS", [128, 1024], BF16)
        QTB = sb("QTB", [128, 1024], BF16)
        YP1 = sb("YP1", [128, 512], BF16)
        UT = sb("UT", [128, 576], F32)
        T12 = sb("T12", [128, 288], F32)
        DT = sb("DT", [128, 512], BF16)
        YH = sb("YH", [128, 2048], BF16)
        CST = sb("CST", [128, 512], BF16)
        WPL = sb("WPL", [128, 512], BF16)
        RC = sb("RC", [128, 64], F32)
        GC = sb("GC", [128, 24], F32)
        ST = sb("ST", [128, 32], F32)
        ONE = sb("ONE", [128, 1], F32)
        banks = [st.enter_context(nc.psum_tensor("B%d" % i, [128, 512], F32)) for i in range(8)]

        KT = BIG[:, 0:32768].rearrange("p (a t) -> p a t", a=4)
        V = BIG[:, 32768:65536].rearrange("p (a c) -> p a c", a=64)
        WUP = BIG[:, 0:32768].rearrange("p (a c) -> p a c", a=8)
        WDN = BIG[:, 32768:65536].rearrange("p (a c) -> p a c", a=32)
        WINV = WIN[:, :].rearrange("p (a c) -> p a c", a=8)
        WOUT = WIN[:, 0:8192].rearrange("p (a c) -> p a c", a=8)
        WGT = WIN[:, 8192:16384].rearrange("p (a c) -> p a c", a=8)
        hnT = HNT[:, :].rearrange("p (a t) -> p a t", a=8)
        yh = HNT[:, 0:2048].rearrange("p (j c t) -> p j c t", j=2, c=8)
        hnT2 = HNT[:, 2048:4096].rearrange("p (a t) -> p a t", a=8)
        gb = [GB[:, k * 516:k * 516 + 513] for k in range(3)]
        sbf = [SBF[:, 0:513], SBF[:, 516:1029]]
        sh = [ABF[:, k * 520:k * 520 + 513] for k in range(3)]
        sq0 = ABF[:, 0:512]
        ats = [ATS[:, 0:512], ATS[:, 512:1024]]
        QT2 = [QTB[:, k * 512:(k + 1) * 512].rearrange("p (a t) -> p a t", a=4) for k in range(2)]
        uT = UT[:, :].rearrange("p (a t) -> p a t", a=4)
        tt = [T12[:, 0:144], T12[:, 144:288]]
        dT = DT[:, :].rearrange("p (a t) -> p a t", a=4)
        ypool = [YH[:, 0:512], YP1[:, :]]
        ysb = YH[:, 512:1024]
        sqT = DT[:, :]
        yT3 = YH[:, 0:1024].rearrange("p (a t) -> p a t", a=8)
        hn = YH[:, 1024:2048]
        WPLE = YH[:, :].rearrange("p (a c) -> p a c", a=2)
        ident = CST[:, 0:128]
        negmask = CST[:, 128:256]
        blockones = CST[:, 256:384]
        negident = CST[:, 384:512]
        wpool = WPL[:, :].rearrange("p (a d) -> p a d", a=4)
        rc = RC[:, :].rearrange("p (a t) -> p a t", a=4)
        Bbf = [b[:, :].bitcast(BF16) for b in banks]
        tmpC = GB[:, 0:1024]
        gainC = SBF[:, 0:1024]
        hnC = ABF[:, 0:1024]
        actT = ATS[:, :].rearrange("p (a t) -> p a t", a=4)
        pbC = QTB[:, 0:256]
        pTC = QTB[:, 256:512].rearrange("p (a t) -> p a t", a=2)
        rC = [UT[:, 0:256], UT[:, 256:512]]
        pC = T12[:, 0:256]

        S.dma("pool", "cst", [(CST[:, :], consts)], writes=["cst"])
        S.dma("pool", "wpl", [(WPL[:, :], w_pool)], writes=["wpool"])
        S.dma("sp", "rc", [(RC[:, :], rc16)], writes=["rc"])
        S.dma("sp", "gc", [(GC[:, :], gcols)], writes=["gc"])
        S.op("pool", lambda e: e.memset(ONE[:, :], 1.0), writes=["one"])
        S.op("pool", lambda e: e.memset(GB[:, 0:1], 1.0), writes=["gb0"])
        S.op("pool", lambda e: e.memset(GB[:, 516:517], 1.0), writes=["gb1"])
        S.op("pool", lambda e: e.memset(GB[:, 1032:1033], 1.0), writes=["gb2"])

        for kc in range(8):
            for hf in range(2):
                i = kc * 2 + hf
                buf = F4[i % 2]
                S.dma("sp", "f4%d" % (i % 2),
                      [(buf[:, :], w_in[kc * 128:(kc + 1) * 128, hf * 1024:(hf + 1) * 1024])],
                      writes=["f4%d" % (i % 2)])
                eng = "act" if i % 2 == 0 else "dve"
                if eng == "act":
                    S.op("act", lambda e, kc=kc, hf=hf, buf=buf: e.activation(
                        out=WINV[:, kc, hf * 1024:(hf + 1) * 1024], in_=buf[:, :], func=AF.Copy,
                        scale=GC[:, kc:kc + 1]),
                        reads=["f4%d" % (i % 2), "gc"], writes=["win"])
                else:
                    S.op("dve", lambda e, kc=kc, hf=hf, buf=buf: e.tensor_scalar(
                        out=WINV[:, kc, hf * 1024:(hf + 1) * 1024], in0=buf[:, :],
                        scalar1=GC[:, kc:kc + 1], scalar2=None, op0=ALU.mult),
                        reads=["f4%d" % (i % 2), "gc"], writes=["win"])

        I32 = mybir.dt.int32
        STI = ST[:, :].bitcast(I32)

        def rstd_thunks(ss_col, c0, scale, tag, iters):
            y, a_, t_, u_ = c0, c0 + 1, c0 + 2, c0 + 3
            col = lambda c: ST[:, c:c + 1]
            coli = lambda c: STI[:, c:c + 1]
            th = []
            th.append(lambda: S.op("dve", lambda e: e.tensor_scalar(
                out=col(a_), in0=col(ss_col), scalar1=scale, scalar2=EPS, op0=ALU.mult, op1=ALU.add),
                reads=["ss" + tag], writes=["a" + tag]))
            th.append(lambda: S.op("dve", lambda e: e.tensor_scalar(
                out=coli(y), in0=coli(a_), scalar1=1, scalar2=None, op0=ALU.logical_shift_right),
                reads=["a" + tag], writes=["r" + tag]))
            th.append(lambda: S.op("dve", lambda e: e.tensor_scalar(
                out=coli(y), in0=coli(y), scalar1=-1, scalar2=0x5f3759df, op0=ALU.mult, op1=ALU.add),
                reads=["r" + tag], writes=["r" + tag]))
            for _ in range(iters):
                th.append(lambda: S.op("dve", lambda e: e.scalar_tensor_tensor(
                    out=col(t_), in0=col(y), scalar=col(a_), in1=col(y), op0=ALU.mult, op1=ALU.mult),
                    reads=["r" + tag, "a" + tag], writes=["t" + tag]))
                th.append(lambda: S.op("dve", lambda e: e.scalar_tensor_tensor(
                    out=col(u_), in0=col(t_), scalar=-0.5, in1=col(y), op0=ALU.mult, op1=ALU.mult),
                    reads=["t" + tag, "r" + tag], writes=["u" + tag]))
                th.append(lambda: S.op("dve", lambda e: e.scalar_tensor_tensor(
                    out=col(y), in0=col(y), scalar=1.5, in1=col(u_), op0=ALU.mult, op1=ALU.add),
                    reads=["u" + tag, "r" + tag], writes=["r" + tag]))
            return th

        def rstd_act(ss_col, t_col, r_col, scale, tag):
            S.op("dve", lambda e: e.tensor_scalar(out=ST[:, t_col:t_col + 1], in0=ST[:, ss_col:ss_col + 1],
                                                  scalar1=scale, scalar2=EPS, op0=ALU.mult, op1=ALU.add),
                 reads=["ss" + tag], writes=["t" + tag])
            S.op("act", lambda e: e.activation(out=ST[:, t_col:t_col + 1], in_=ST[:, t_col:t_col + 1],
                                               func=AF.Sqrt),
                 reads=["t" + tag], writes=["t" + tag])
            S.op("dve", lambda e: e.reciprocal(out=ST[:, r_col:r_col + 1], in_=ST[:, t_col:t_col + 1]),
                 reads=["t" + tag], writes=["r" + tag])

        xcnt = [0]
        step_ctr = [0]

        junkA = SBF[:, 516:1028].bitcast(BF16)

        def proj_kv(m):
            out = []
            tag = "A"
            for bi in range(4):
                pos = 4 * m + bi
                xi = xcnt[0] % 2
                xcnt[0] += 1
                xt = F4[xi]
                xn = "f4%d" % xi
                out.append(lambda xt=xt, xn=xn, pos=pos: S.dma(
                    "sp", xn, [(xt[:, :], xk[pos * 128:(pos + 1) * 128, :])], writes=[xn]))
                out.append(lambda xt=xt, xn=xn, bi=bi: S.op(
                    "act", lambda e: e.activation(out=junkA, in_=xt[:, :], func=AF.Square,
                                                  accum_out=ST[:, bi:bi + 1]),
                    reads=[xn], writes=["junkA", "ssA%d" % bi]))
            out.append(lambda: S.op(
                "dve", lambda e: e.tensor_scalar(out=ST[:, 28:32], in0=ST[:, 0:4], scalar1=1.0 / D, scalar2=EPS,
                                                 op0=ALU.mult, op1=ALU.add),
                reads=["ssA0", "ssA1", "ssA2", "ssA3"], writes=["tA"]))
            out.append(lambda: S.op(
                "act", lambda e: e.activation(out=ST[:, 28:32], in_=ST[:, 28:32], func=AF.Sqrt),
                reads=["tA"], writes=["tA"]))
            out.append(lambda: S.op(
                "dve", lambda e: e.reciprocal(out=ST[:, 24:28], in_=ST[:, 28:32]),
                reads=["tA"], writes=["rA"]))
            per_block = []
            for bi in range(4):
                th = []
                pos = 4 * m + bi
                xi = xcnt[0] % 2
                xcnt[0] += 1
                xt = F4[xi]
                xn = "f4%d" % xi
                th.append(lambda xt=xt, xn=xn, pos=pos: S.dma(
                    "sp", xn, [(xt[:, :], xk[pos * 128:(pos + 1) * 128, :])], writes=[xn]))
                th.append(lambda xt=xt, xn=xn, bi=bi: S.op(
                    "act", lambda e: e.activation(out=hn, in_=xt[:, :], func=AF.Copy, scale=ST[:, 24 + bi:25 + bi]),
                    reads=[xn, "rA"], writes=["hn"]))

                for h2 in range(2):
                    def tr4(e, h2=h2):
                        inst = None
                        for kc in range(4 * h2, 4 * h2 + 4):
                            inst = e.transpose(Bbf[0][:, kc * 128:(kc + 1) * 128],
                                               hn[:, kc * 128:(kc + 1) * 128], ident)
                        return inst
                    th.append(lambda tr4=tr4: S.op("pe", tr4, reads=["hn", "cst"], writes=["b0"]))
                th.append(lambda bi=bi: S.op(
                    "dve", lambda e: e.tensor_copy(out=hnT[:, :, bi * 128:(bi + 1) * 128],
                                                   in_=Bbf[0][:, :].rearrange("p (a t) -> p a t", a=8)),
                    reads=[], writes=["b0", "hnT%d" % bi]))
                for q4 in range(4):
                    def vmm(e, bi=bi, q4=q4):
                        inst = None
                        for kc in range(2 * q4, 2 * q4 + 2):
                            inst = e.matmul(banks[1][:, :], lhsT=hnT[:, kc, bi * 128:(bi + 1) * 128],
                                            rhs=WINV[:, kc, 1536:2048], start=(kc == 0), stop=(kc == 7))
                        return inst
                    th.append(lambda vmm=vmm, bi=bi: S.op("pe", vmm, reads=["hnT%d" % bi, "win"], writes=["b1"]))
                th.append(lambda pos=pos, m=m: S.op(
                    "act", lambda e: e.activation(out=V[:, pos, :], in_=banks[1][:, :], func=AF.Copy),
                    reads=[], writes=["b1", "V%d" % m]))
                per_block.append(th)
            stagger = 6
            keyed = []
            for bi, th in enumerate(per_block):
                for k, t_ in enumerate(th):
                    keyed.append((k + bi * stagger, bi, k, t_))
            keyed.sort(key=lambda z: (z[0], z[1]))
            out += [z[3] for z in keyed]
            allT = ["hnT%d" % bi for bi in range(4)]
            for pr in range(4):
                for q4 in range(4):
                    def kmm(e, pr=pr, q4=q4):
                        inst = None
                        for kc in range(2 * q4, 2 * q4 + 2):
                            inst = e.matmul(banks[1][:, :],
                                            lhsT=WINV[:, kc, 1024 + pr * 128:1024 + (pr + 1) * 128],
                                            rhs=hnT[:, kc, :], start=(kc == 0), stop=(kc == 7))
                        return inst
                    out.append(lambda kmm=kmm: S.op("pe", kmm, reads=allT + ["win"], writes=["b1"]))
                out.append(lambda pr=pr, m=m: S.op(
                    "dve", lambda e: e.tensor_copy(out=KT[:, pr, m * 512:(m + 1) * 512], in_=banks[1][:, :]),
                    reads=[], writes=["b1", "KT%d" % m]))
            return out

        def proj_own(m):
            qp = m % 2
            th = []

            def t_qp(pr):
                def qmm(e):
                    inst = None
                    for kc in range(8):
                        inst = e.matmul(banks[1][:, pr * 128:(pr + 1) * 128],
                                        lhsT=WINV[:, kc, 512 + pr * 128:512 + (pr + 1) * 128],
                                        rhs=hnT[:, kc, 0:128], start=(kc == 0), stop=(kc == 7))
                    return inst
                S.op("pe", qmm, reads=["hnT0", "win"], writes=["b1"])
            for pr in range(4):
                th.append(lambda pr=pr: t_qp(pr))
            th.append(lambda: S.op(
                "dve", lambda e: e.tensor_copy(out=QTB[:, qp * 512:(qp + 1) * 512], in_=banks[1][:, :]),
                reads=[], writes=["b1", "QT%d" % qp]))

            def t_u(rd):
                def umm(e):
                    inst = None
                    for q in range(2):
                        gi = rd * 2 + q
                        for kc in range(8):
                            inst = e.matmul(banks[1][:, q * 144:(q + 1) * 144],
                                            lhsT=WINV[:, kc, gi * 128:(gi + 1) * 128],
                                            rhs=hnT[:, kc, 0:144], start=(kc == 0), stop=(kc == 7))
                    return inst
                S.op("pe", umm, reads=["hnT0", "hnT1", "win"], writes=["b1"])
                S.op("act", lambda e: e.activation(out=UT[:, rd * 288:(rd + 1) * 288],
                                                   in_=banks[1][:, 0:288], func=AF.Copy),
                     reads=[], writes=["b1", "uT"])
            th.append(lambda: t_u(0))
            th.append(lambda: t_u(1))

            def t_mix(gi):
                a_ = uT[:, gi, :]
                w = 2 ** (gi + 1)
                S.op("dve", lambda e: e.tensor_tensor(out=tt[0][:, 0:143], in0=a_[:, 0:143],
                                                      in1=a_[:, 1:144], op=ALU.add),
                     reads=["uT"], writes=["tt0"])
                cur, ln, sh_ = 0, 143, 2
                for lvl in range(gi):
                    nl = ln - sh_
                    S.op("dve", lambda e, cur=cur, nl=nl, sh_=sh_: e.tensor_tensor(
                        out=tt[1 - cur][:, 0:nl], in0=tt[cur][:, 0:nl], in1=tt[cur][:, sh_:sh_ + nl], op=ALU.add),
                        reads=["tt%d" % cur], writes=["tt%d" % (1 - cur)])
                    cur, ln, sh_ = 1 - cur, nl, sh_ * 2
                S.op("dve", lambda e, cur=cur: e.scalar_tensor_tensor(
                    out=dT[:, gi, :], in0=tt[cur][:, 0:128], scalar=1.0 / w, in1=a_[:, 0:128],
                    op0=ALU.mult, op1=ALU.subtract),
                    reads=["tt%d" % cur, "uT"], writes=["dT"])
                if m == 15:
                    S.op("dve", lambda e, cur=cur: e.tensor_tensor(
                        out=tt[1 - cur][:, 0:16], in0=tt[cur][:, 112:128], in1=rc[:, gi, :], op=ALU.mult),
                        reads=["tt%d" % cur, "rc"], writes=["tt%d" % (1 - cur)])
                    S.op("dve", lambda e, cur=cur: e.tensor_tensor(
                        out=dT[:, gi, 112:128], in0=tt[1 - cur][:, 0:16], in1=a_[:, 112:128], op=ALU.subtract),
                        reads=["tt%d" % (1 - cur), "uT"], writes=["dT"])
            for gi in range(4):
                th.append(lambda gi=gi: t_mix(gi))

            def t_p():
                def pmm(e):
                    inst = None
                    for gi in range(4):
                        inst = e.matmul(banks[1][:, gi * 128:(gi + 1) * 128], lhsT=wpool[:, gi, :],
                                        rhs=dT[:, gi, :], start=True, stop=True)
                    return inst
                S.op("pe", pmm, reads=["dT", "wpool"], writes=["b1"])
                S.op("act", lambda e: e.activation(out=ypool[qp], in_=banks[1][:, :], func=AF.Copy),
                     reads=[], writes=["b1", "yp%d" % qp])
            th.append(t_p)
            return th

        def attention(m, bg):
            sp_ = (15 - m) % 2
            oT = banks[6 + sp_]
            oTn = "oT%d" % sp_
            nch = 16 - m
            steps = []
            for pr in range(4):
                for c in range(nch):
                    for hh in range(2):
                        steps.append((pr, hh, c))
            K = len(steps)
            base = step_ctr[0]
            step_ctr[0] += K
            QT = QT2[m % 2]
            QTn = "QT%d" % (m % 2)

            def emit_qk(i):
                pr, hh, c = steps[i]
                par = (base + i) % 2
                g3 = (base + i) % 3
                lo, hi = hh * 64, hh * 64 + 64
                k0 = (4 * m + 4 * c) * 128

                def f(e):
                    inst = e.matmul(banks[2 + par][:, :], lhsT=QT[lo:hi, pr, :], rhs=KT[lo:hi, pr, k0:k0 + 512],
                                    start=True, stop=(c != 0))
                    if c == 0:
                        inst = e.matmul(banks[2 + par][:, 0:128], lhsT=ident, rhs=negmask,
                                        start=False, stop=True)
                    return inst
                S.op("pe", f, reads=[QTn, "KT%d" % (m + c), "cst"], writes=["z%d" % par])
                S.op("act", lambda e: e.activation(out=gb[g3][:, 1:513], in_=banks[2 + par][:, :],
                                                   func=AF.Sigmoid, scale=-0.125),
                     reads=[], writes=["z%d" % par, "gb%d" % g3])

            def emit_scan(i):
                pr, hh, c = steps[i]
                g3 = (base + i) % 3
                if c == 0:
                    init = 1.0
                    rds = ["gb%d" % g3, "one"]
                else:
                    p3 = (base + i - 2) % 3
                    init = sh[p3][:, 512:513]
                    rds = ["gb%d" % g3, "one", "sh%d" % p3]
                S.op("dve", lambda e: e.tensor_tensor_scan(
                    out=sh[g3], data0=gb[g3], data1=ONE[:, 0:1].to_broadcast([128, 513]),
                    initial=init, op0=ALU.mult, op1=ALU.mult),
                    reads=rds, writes=["sh%d" % g3])

            def emit_tr(i):
                par = (base + i) % 2
                s3 = (base + i) % 3

                def f(e):
                    inst = None
                    for j in range(4):
                        e.matmul(banks[4 + par][:, j * 128:(j + 1) * 128], lhsT=sh[s3][:, j * 128:j * 128 + 128],
                                 rhs=ident, start=True, stop=False)
                        inst = e.matmul(banks[4 + par][:, j * 128:(j + 1) * 128],
                                        lhsT=sh[s3][:, j * 128 + 1:j * 128 + 129],
                                        rhs=negident, start=False, stop=True)
                    return inst
                S.op("pe", f, reads=["sh%d" % s3, "cst"], writes=["atp%d" % par])
                S.op("act", lambda e: e.activation(out=ats[par], in_=banks[4 + par][:, :], func=AF.Copy),
                     reads=[], writes=["atp%d" % par, "ats%d" % par])

            def emit_av(i):
                pr, hh, c = steps[i]
                par = (base + i) % 2
                head = 2 * pr + hh
                lo, hi = hh * 64, hh * 64 + 64

                def f(e):
                    inst = None
                    for j in range(4):
                        pos = 4 * m + 4 * c + j
                        inst = e.matmul(oT[lo:hi, pr * 128:(pr + 1) * 128],
                                        lhsT=V[:, pos, head * 64:(head + 1) * 64],
                                        rhs=ats[par][:, j * 128:(j + 1) * 128],
                                        start=(c == 0 and j == 0), stop=(c == nch - 1 and j == 3))
                    return inst
                S.op("pe", f, reads=["ats%d" % par, "V%d" % (m + c)], writes=[oTn])

            n_it = K + 5
            done = 0
            for it, i in enumerate(range(-3, K + 2)):
                if 0 <= i < K:
                    emit_scan(i)
                if 0 <= i + 3 < K:
                    emit_qk(i + 3)
                if 0 <= i - 1 < K:
                    emit_tr(i - 1)
                if 0 <= i - 2 < K:
                    emit_av(i - 2)
                tgt = (len(bg) * (it + 1)) // n_it
                while done < tgt:
                    bg[done]()
                    done += 1
            while done < len(bg):
                bg[done]()
                done += 1

        def tail_thunks(m):
            sp_ = (15 - m) % 2
            oT = banks[6 + sp_]
            oTn = "oT%d" % sp_
            qp = m % 2
            th = []
            th.append(lambda: S.op("act", lambda e: e.activation(out=sqT, in_=oT[:, :], func=AF.Square),
                                   reads=[], writes=[oTn, "dT"]))
            th.append(lambda: S.op("pe", lambda e: e.matmul(banks[1][:, :], lhsT=blockones, rhs=sqT,
                                                            start=True, stop=True),
                                   reads=["dT", "cst"], writes=["b1"]))
            th.append(lambda: S.op("dve", lambda e: e.tensor_scalar(out=sbf[0][:, 0:512], in0=banks[1][:, :],
                                                                    scalar1=EPS, scalar2=None, op0=ALU.add),
                                   reads=[], writes=["b1", "sb0"]))
            th.append(lambda: S.op("act", lambda e: e.activation(out=sbf[0][:, 0:512], in_=sbf[0][:, 0:512],
                                                                 func=AF.Sqrt),
                                   reads=["sb0"], writes=["sb0"]))
            th.append(lambda: S.op("dve", lambda e: e.reciprocal(out=sbf[0][:, 0:512], in_=sbf[0][:, 0:512]),
                                   reads=["sb0"], writes=["sb0"]))
            th.append(lambda: S.op("dve", lambda e: e.tensor_tensor(out=ysb, in0=oT[:, :],
                                                                    in1=sbf[0][:, 0:512], op=ALU.mult),
                                   reads=["sb0"], writes=[oTn, "ysb"]))
            th.append(lambda: S.dma("sp", "yscr", [(yscr[m][:, 0:512], ypool[qp]), (yscr[m][:, 512:1024], ysb)],
                                    reads=["yp%d" % qp, "ysb"], writes=["yscr%d" % m]))
            return th

        stg = [0]

        def stage(dst_view, src_ap, gcol, wname, extra=()):
            i = stg[0]
            stg[0] += 1
            buf = F4[i % 2]
            bn = "f4%d" % (i % 2)
            S.dma("sp", bn, [(buf[:, :], src_ap)], writes=[bn])
            if i % 2 == 0:
                S.op("act", lambda e: e.activation(out=dst_view, in_=buf[:, :], func=AF.Copy,
                                                   scale=GC[:, gcol:gcol + 1]),
                     reads=[bn, "gc"], writes=[wname] + list(extra))
            else:
                S.op("dve", lambda e: e.tensor_scalar(out=dst_view, in0=buf[:, :], scalar1=GC[:, gcol:gcol + 1],
                                                      scalar2=None, op0=ALU.mult),
                     reads=[bn, "gc"], writes=[wname] + list(extra))

        def win_loads():
            th = []
            for kc in range(8):
                th.append(lambda kc=kc: stage(WOUT[:, kc, :], w_out[kc * 128:(kc + 1) * 128, :], 8 + kc, "win"))
            th.append(lambda: S.dma(
                "pool", "wgt", [(WGT[:, kc * 4:(kc + 1) * 4, :],
                                 w_gate[kc * 512:(kc + 1) * 512, :].rearrange("(c p) n -> p c n", p=128))
                                for kc in range(2)], writes=["win"]))
            return th

        m_lo = 16 - n_groups
        for t_ in proj_kv(15) + proj_own(15):
            t_()
        win_done = False
        prev_tail = []
        for m_ in range(15, m_lo - 1, -1):
            bg = list(prev_tail)
            if m_ - 1 >= m_lo:
                bg += proj_kv(m_ - 1) + proj_own(m_ - 1)
            elif do_c:
                bg += win_loads()
                win_done = True
            attention(m_, bg)
            prev_tail = tail_thunks(m_)
        for t_ in prev_tail:
            t_()

        if not do_c:
            o = S.dma("sp", "yout", [(y_out[0:128, 0:512], UT[:, 0:512])], reads=["uT"])
            S.finish([o])
            S.emit(nc, st)
            return nc

        allKV = ["KT%d" % i for i in range(16)] + ["V%d" % i for i in range(16)]
        if not win_done:
            for t_ in win_loads():
                t_()
        S.dma("pool", "wple", [(WPLE[:, :, :], w_ple.rearrange("(c p) n -> p c n", p=128))],
              reads=[], writes=["yp0", "ysb", "hn"])
        npiece = 0
        for q in range(4):
            for kc in range(8):
                stage(WUP[:, kc, q * 1024:(q + 1) * 1024],
                      w_up[kc * 128:(kc + 1) * 128, q * 1024:(q + 1) * 1024], 16 + kc, "wup%d" % q,
                      extra=(allKV if npiece < 2 else ()))
                npiece += 1
        for c4 in range(8):
            S.dma("pool", "wdn%d" % c4,
                  [(WDN[:, c4 * 4:(c4 + 1) * 4, :],
                    w_down[c4 * 512:(c4 + 1) * 512, :].rearrange("(c p) n -> p c n", p=128))],
                  writes=(allKV if c4 == 0 else []) + ["wdn%d" % c4])

        outs = []
        gidx = {"mix": 0, "mlp": 1, "ple": 2}

        cur_gain = [None]

        def load_gain(which):
            if cur_gain[0] == which:
                return
            cur_gain[0] = which
            S.dma("sp", "gain", [(gainC, g_post[gidx[which]:gidx[which] + 1, :].partition_broadcast(128))],
                  writes=["gain"])

        def norm_residual(src_banks, xt, xn, which, tagc):
            for hf in range(2):
                S.op("act", lambda e, hf=hf: e.activation(out=hnC[:, hf * 512:(hf + 1) * 512],
                                                          in_=banks[src_banks[hf]][:, :], func=AF.Square,
                                                          accum_out=ST[:, 4 + hf:5 + hf]),
                     reads=[], writes=["b%d" % src_banks[hf], "hnC", "ssq%d" % hf])
            S.op("dve", lambda e: e.tensor_tensor(out=ST[:, 6:7], in0=ST[:, 4:5], in1=ST[:, 5:6], op=ALU.add),
                 reads=["ssq0", "ssq1"], writes=["ssN"])
            rstd_act(6, 7, 8, 1.0 / D, "N")
            load_gain(which)
            for hf in range(2):
                S.op("dve", lambda e, hf=hf: e.scalar_tensor_tensor(
                    out=tmpC[:, hf * 512:(hf + 1) * 512], in0=banks[src_banks[hf]][:, :], scalar=ST[:, 8:9],
                    in1=gainC[:, hf * 512:(hf + 1) * 512], op0=ALU.mult, op1=ALU.mult),
                    reads=["rN", "gain"], writes=["b%d" % src_banks[hf], "tmpC"])
            if xt is not None:
                S.op("dve", lambda e: e.tensor_tensor(out=xt[:, :], in0=xt[:, :], in1=tmpC, op=ALU.add),
                     reads=["tmpC", xn], writes=[xn])

        def to_T(src_ap, dst3, nk, src_name, dst_name):
            def f(e):
                inst = None
                for kc in range(nk):
                    inst = e.transpose(Bbf[0][:, kc * 128:(kc + 1) * 128], src_ap[:, kc * 128:(kc + 1) * 128], ident)
                return inst
            S.op("pe", f, reads=[src_name, "cst"], writes=["b0"])
            S.op("dve", lambda e: e.tensor_copy(
                out=dst3, in_=Bbf[0][:, 0:nk * 128].rearrange("p (a t) -> p a t", a=nk)),
                reads=[], writes=["b0", dst_name])

        for t in range(8):
            accb = [[4, 5], [6, 7]]
            for j in range(2):
                mblk = 2 * t + j
                if t == 0:
                    S.dma("sp", "yh%d" % j, [(HNT[:, j * 1024:(j + 1) * 1024], yscr[mblk])],
                          reads=["yscr%d" % mblk], writes=["yh%d" % j])
                S.dma("sp", "f4%d" % j, [(F4[j][:, :], xk[(4 * mblk) * 128:(4 * mblk + 1) * 128, :])],
                      writes=["f4%d" % j])
            for j in range(2):
                def f(e, j=j):
                    inst = None
                    for hf in range(2):
                        for kc in range(8):
                            inst = e.matmul(banks[accb[j][hf]][:, :], lhsT=yh[:, j, kc, :],
                                            rhs=WOUT[:, kc, hf * 512:(hf + 1) * 512],
                                            start=(kc == 0), stop=(kc == 7))
                    return inst
                S.op("pe", f, reads=["yh%d" % j, "win"], writes=["b%d" % accb[j][0], "b%d" % accb[j][1]])
                norm_residual(accb[j], F4[j], "f4%d" % j, "mix", "m")
            if t < 7:
                for j in range(2):
                    nblk = 2 * (t + 1) + j
                    S.dma("sp", "yh%d" % j, [(HNT[:, j * 1024:(j + 1) * 1024], yscr[nblk])],
                          reads=["yscr%d" % nblk], writes=["yh%d" % j])
            for j in range(2):
                S.op("act", lambda e, j=j: e.activation(out=hnC, in_=F4[j][:, :], func=AF.Square,
                                                        accum_out=ST[:, 14:15]),
                     reads=["f4%d" % j], writes=["hnC", "ssM"])
                rstd_act(14, 15, 16, 1.0 / D, "M")
                S.op("act", lambda e, j=j: e.activation(out=hnC, in_=F4[j][:, :], func=AF.Copy,
                                                        scale=ST[:, 16:17]),
                     reads=["f4%d" % j, "rM"], writes=["hnC"])
                to_T(hnC, hnT2[:, :, j * 128:(j + 1) * 128], 8, "hnC", "hnT2")
            def up(fc):
                ub = 2 + fc % 2

                def f(e):
                    inst = None
                    for kc in range(8):
                        inst = e.matmul(banks[ub][:, 0:256], lhsT=WUP[:, kc, fc * 128:(fc + 1) * 128],
                                        rhs=hnT2[:, kc, :], start=(kc == 0), stop=(kc == 7))
                    return inst
                S.op("pe", f, reads=["hnT2", "wup%d" % (fc // 8)], writes=["b%d" % ub])
                rr = rC[fc % 2]
                S.op("act", lambda e: e.activation(out=rr, in_=banks[ub][:, 0:256], func=AF.Relu),
                     reads=[], writes=["b%d" % ub, "r%d" % (fc % 2)])
                S.op("dve", lambda e: e.tensor_tensor(out=actT[:, fc % 4, :], in0=rr, in1=rr, op=ALU.mult),
                     reads=["r%d" % (fc % 2)], writes=["act%d" % (fc % 4)])

            def down(fc):
                def f(e):
                    inst = None
                    for j in range(2):
                        for hf in range(2):
                            inst = e.matmul(banks[accb[j][hf]][:, :], lhsT=actT[:, fc % 4, j * 128:(j + 1) * 128],
                                            rhs=WDN[:, fc, hf * 512:(hf + 1) * 512],
                                            start=(fc == 0), stop=(fc == 31))
                    return inst
                S.op("pe", f, reads=["act%d" % (fc % 4), "wdn%d" % (fc // 4)], writes=["b4", "b5", "b6", "b7"])

            for i in range(34):
                if i < 32:
                    up(i)
                if 0 <= i - 2 < 32:
                    down(i - 2)
            for j in range(2):
                norm_residual(accb[j], F4[j], "f4%d" % j, "mlp", "d")
            for j in range(2):
                mblk = 2 * t + j
                S.op("act", lambda e, j=j: e.activation(out=hnC, in_=F4[j][:, :], func=AF.Copy),
                     reads=["f4%d" % j], writes=["hnC"])
                to_T(hnC, hnT2[:, :, j * 128:(j + 1) * 128], 8, "hnC", "hnT2")
                if t == 0 and j == 0:
                    S.dma("sp", "pC", [(pC, p_own[mblk * 128:(mblk + 1) * 128, :])], writes=["pC"])
                S.op("dve", lambda e: e.tensor_copy(out=pbC, in_=pC), reads=["pC"], writes=["pbC"])
                nxt = mblk + 1
                if nxt < 16:
                    S.dma("sp", "pC", [(pC, p_own[nxt * 128:(nxt + 1) * 128, :])], writes=["pC"])
                to_T(pbC, pTC, 2, "pbC", "pTC")

                def f(e, j=j):
                    inst = None
                    for hf in range(2):
                        for kc in range(2):
                            inst = e.matmul(banks[2 + hf][:, :], lhsT=pTC[:, kc, :],
                                            rhs=WPLE[:, kc, hf * 512:(hf + 1) * 512],
                                            start=(kc == 0), stop=(kc == 1))
                    return inst
                S.op("pe", f, reads=["pTC", "ysb", "hn"], writes=["b2", "b3"])
                norm_residual([2, 3], None, None, "ple", "p")

                def fg(e, j=j):
                    inst = None
                    for hf in range(2):
                        for kc in range(8):
                            inst = e.matmul(banks[accb[j][hf]][:, :], lhsT=hnT2[:, kc, j * 128:(j + 1) * 128],
                                            rhs=WGT[:, kc, hf * 512:(hf + 1) * 512],
                                            start=(kc == 0), stop=(kc == 7))
                    return inst
                S.op("pe", fg, reads=["hnT2", "win"], writes=["b%d" % accb[j][0], "b%d" % accb[j][1]])
                cur_gain[0] = None
                for hf in range(2):
                    S.op("act", lambda e, j=j, hf=hf: e.activation(
                        out=gainC[:, hf * 512:(hf + 1) * 512], in_=banks[accb[j][hf]][:, :], func=AF.Sigmoid),
                        reads=[], writes=["b%d" % accb[j][hf], "gain"])
                S.op("dve", lambda e: e.tensor_tensor(out=tmpC, in0=tmpC, in1=gainC, op=ALU.mult),
                     reads=["tmpC", "gain"], writes=["tmpC"])
                S.op("dve", lambda e, j=j: e.tensor_tensor(out=F4[j][:, :], in0=F4[j][:, :], in1=tmpC, op=ALU.add),
                     reads=["tmpC", "f4%d" % j], writes=["f4%d" % j])
                outs.append(S.dma("sp", "yout%d" % j, [(y_out[mblk * 128:(mblk + 1) * 128, :], F4[j][:, :])],
                                  reads=["f4%d" % j]))
        S.finish(outs)
        S.emit(nc, st)
    return nc


def _consts():
    ident = np.eye(128, dtype=np.float32)
    p = np.arange(128)[:, None]
    i = np.arange(128)[None, :]
    negmask = np.where(i <= p, NEG, 0.0).astype(np.float32)
    blockones = np.where((p // 64) == (i // 64), 1.0 / 64.0, 0.0).astype(np.float32)
    return np.concatenate([ident, negmask, blockones, -ident], axis=1)


def _rc16(g):
    rc = np.zeros((4, 16), np.float32)
    for gi in range(4):
        w = 2 ** (gi + 1)
        for q in range(16):
            i = 112 + q
            if g == 0:
                cnt = min(128 - i, w)
            else:
                cnt = w
            rc[gi, q] = 1.0 / cnt
    return np.ascontiguousarray(np.broadcast_to(rc.reshape(1, 64), (128, 64)))


_PROG = {}


def _prepare(x, p, g_mix_pre, w_in, w_pool, pool_scale, g_sb, w_out, g_mix_post,
             g_mlp_pre, w_up, w_down, g_mlp_post, w_ple_gate, w_ple_proj, g_ple):
    f = np.float32
    x = np.asarray(x, f)
    p = np.asarray(p, f)

    def cols(v):
        return np.asarray(v, f).reshape(8, 128).T

    gcat = np.concatenate([np.asarray(pool_scale, f)[0], np.asarray(g_sb, f)[0]])
    gcols = np.ascontiguousarray(np.concatenate(
        [cols(np.asarray(g_mix_pre, f)[0]), cols(gcat), cols(np.asarray(g_mlp_pre, f)[0])], axis=1))
    g_post = np.ascontiguousarray(np.stack([np.asarray(g_mix_post, f)[0], np.asarray(g_mlp_post, f)[0],
                                            np.asarray(g_ple, f)[0]]))
    wpool = np.ascontiguousarray(np.asarray(w_pool, f)[0].transpose(1, 0, 2).reshape(128, 512))
    shared = {
        "w_in": np.ascontiguousarray(np.asarray(w_in, f)[0]),
        "w_pool": wpool,
        "w_out": np.ascontiguousarray(np.asarray(w_out, f)[0]),
        "w_up": np.ascontiguousarray(np.asarray(w_up, f)[0]),
        "w_down": np.ascontiguousarray(np.asarray(w_down, f)[0]),
        "w_gate": np.ascontiguousarray(np.asarray(w_ple_gate, f)[0]),
        "w_ple": np.ascontiguousarray(np.asarray(w_ple_proj, f)[0]),
        "gcols": gcols,
        "g_post": g_post,
        "consts": _consts(),
    }
    in_maps = []
    tok_maps = []
    for c in range(8):
        b, g = c // 4, c % 4
        t0 = (61 + g) * 128 - 1
        xk = np.zeros((S_LEN, D), f)
        n = t0 + 1
        xk[:n] = x[b, t0::-1][:n]
        toks = np.empty(2048, np.int64)
        for m in range(16):
            qb = 60 + g - 4 * m
            toks[m * 128:(m + 1) * 128] = qb * 128 + 127 - np.arange(128)
        tok_maps.append((b, toks))
        d = dict(shared)
        d["xk"] = xk
        d["p_own"] = np.ascontiguousarray(p[0, b][toks])
        d["rc16"] = _rc16(g)
        in_maps.append(d)
    return in_maps, tok_maps


def kernel(**inputs):
    if "nc" not in _PROG:
        _PROG["nc"] = build_program()
    nc = _PROG["nc"]
    in_maps, tok_maps = _prepare(**inputs)
    res = run_bass_kernel_spmd(nc, in_maps, core_ids=list(range(8)))
    f = np.float32
    out = np.empty((2, S_LEN, D), f)
    for c in range(8):
        b, toks = tok_maps[c]
        out[b, toks] = res.results[c]["y_out"]
    return out
```
